# Optimizing a Trainium2 kernel written in Bass

```python
import jax, jax.numpy as jnp
from jax import lax
import numpy as np

D_MODEL = 1024
BATCH = 8
SEQ = 4096
DEPTH = 1

PLE_DIM = 256
DN_HEADS = 8
DN_HEAD_K = 128
DN_HEAD_V = 128
DN_QK = DN_HEADS * DN_HEAD_K
DN_V = DN_HEADS * DN_HEAD_V
DN_CONV = 4
DN_CHUNK = 64
CF_CH = 1024
CF_CONV = 31
N_BRANCH = 2
IN_SPLITS = (2 * DN_QK + DN_V, DN_V, DN_HEADS, DN_HEADS, 2 * CF_CH, CF_CH, N_BRANCH * D_MODEL)
N_IN = 2 * DN_QK + 2 * DN_V + 2 * DN_HEADS + 3 * CF_CH + N_BRANCH * D_MODEL
EPS = 1e-6

kernel_name = "hybrid_gated_deltanet_conformer_block"


def _rmsnorm(x, g):
    xf = x.astype(jnp.float32)
    y = xf * lax.rsqrt(jnp.mean(xf * xf, axis=-1, keepdims=True) + EPS)
    return (y * g.astype(jnp.float32)).astype(x.dtype)


def _layernorm(x, g, b):
    xf = x.astype(jnp.float32)
    xc = xf - jnp.mean(xf, axis=-1, keepdims=True)
    y = xc * lax.rsqrt(jnp.mean(xc * xc, axis=-1, keepdims=True) + EPS)
    return (y * g.astype(jnp.float32) + b.astype(jnp.float32)).astype(x.dtype)


def _l2norm(x):
    xf = x.astype(jnp.float32)
    return xf * lax.rsqrt(jnp.sum(xf * xf, axis=-1, keepdims=True) + EPS)


def _causal_dwconv(x, w):
    width = w.shape[0]
    return lax.conv_general_dilated(
        x, w[:, None, :].astype(x.dtype), window_strides=(1,), padding=[(width - 1, 0)],
        dimension_numbers=("NWC", "WIO", "NWC"), feature_group_count=x.shape[-1])


def _chunk_gated_delta_rule(q, k, v, log_decay, beta):
    b, t, h, dk = q.shape
    dv = v.shape[-1]
    c = DN_CHUNK
    n = t // c

    def chunks(a):
        return a.astype(jnp.float32).reshape(b, n, c, h, -1).transpose(0, 3, 1, 2, 4)

    q = chunks(q) * (dk ** -0.5)
    k = chunks(k)
    v = chunks(v)
    g = jnp.cumsum(chunks(log_decay[..., None])[..., 0], axis=-1)
    bt = chunks(beta[..., None])[..., 0]
    incl = jnp.tril(jnp.ones((c, c), dtype=bool))
    strict = jnp.tril(jnp.ones((c, c), dtype=bool), -1)
    decay = jnp.where(incl, jnp.exp(jnp.where(incl, g[..., :, None] - g[..., None, :], 0.0)), 0.0)
    kb = k * bt[..., None]
    a = jnp.where(strict, jnp.einsum("bhncd,bhnsd->bhncs", kb, k) * decay, 0.0)
    eye = jnp.eye(c, dtype=jnp.float32)
    rhs = jnp.concatenate([v * bt[..., None], kb * jnp.exp(g)[..., None]], axis=-1)
    sol = lax.linalg.triangular_solve(a + eye, rhs, left_side=True, lower=True, unit_diagonal=True)
    u, w = sol[..., :dv], sol[..., dv:]
    qk = jnp.where(incl, jnp.einsum("bhncd,bhnsd->bhncs", q, k) * decay, 0.0)
    g_last = g[..., -1:]
    q_g = q * jnp.exp(g)[..., None]
    k_d = k * jnp.exp(g_last - g)[..., None]
    xs = (jnp.moveaxis(q_g, 2, 0), jnp.moveaxis(k_d, 2, 0), jnp.moveaxis(u, 2, 0),
          jnp.moveaxis(w, 2, 0), jnp.moveaxis(qk, 2, 0), jnp.moveaxis(jnp.exp(g_last), 2, 0))

    def step(state, inp):
        q_i, k_i, u_i, w_i, qk_i, gl_i = inp
        v_new = u_i - jnp.einsum("bhcd,bhde->bhce", w_i, state)
        o_i = jnp.einsum("bhcd,bhde->bhce", q_i, state) + jnp.einsum("bhcs,bhse->bhce", qk_i, v_new)
        state = state * gl_i[..., None] + jnp.einsum("bhcd,bhce->bhde", k_i, v_new)
        return state, o_i

    s0 = jnp.zeros((b, h, dk, dv), jnp.float32)
    _, o = lax.scan(step, s0, xs)
    return o.transpose(1, 0, 3, 2, 4).reshape(b, t, h, dv)


def setup_inputs(seed: int = 0) -> dict:
    key = jax.random.key(seed)
    ks = jax.random.split(key, 20)
    f32 = jnp.float32

    def nrm(k, shape, scale):
        return jax.random.normal(k, shape, f32) * scale

    dt = jnp.exp(jax.random.uniform(ks[5], (DEPTH, DN_HEADS), f32, np.log(1e-3), np.log(1e-1)))
    return {
        "x": nrm(ks[0], (BATCH, SEQ, D_MODEL), 1.0),
        "p": nrm(ks[1], (DEPTH, BATCH, SEQ, PLE_DIM), 1.0),
        "mix_norm_g": 1.0 + nrm(ks[2], (DEPTH, D_MODEL), 0.02),
        "w_in": nrm(ks[3], (DEPTH, D_MODEL, N_IN), D_MODEL ** -0.5),
        "dn_conv_w": nrm(ks[4], (DEPTH, DN_CONV, 2 * DN_QK + DN_V), DN_CONV ** -0.5),
        "dn_a_log": jnp.log(jax.random.uniform(ks[6], (DEPTH, DN_HEADS), f32, 1.0, 16.0)),
        "dn_dt_bias": dt + jnp.log(-jnp.expm1(-dt)),
        "dn_out_norm_g": 1.0 + nrm(ks[7], (DEPTH, DN_HEAD_V), 0.02),
        "w_dn_out": nrm(ks[8], (DEPTH, DN_V, D_MODEL), DN_V ** -0.5),
        "cf_dw_w": nrm(ks[9], (DEPTH, CF_CONV, CF_CH), CF_CONV ** -0.5),
        "cf_dw_b": nrm(ks[10], (DEPTH, CF_CH), 0.02),
        "cf_ln_g": 1.0 + nrm(ks[11], (DEPTH, CF_CH), 0.02),
        "cf_ln_b": nrm(ks[12], (DEPTH, CF_CH), 0.02),
        "w_cf_out": nrm(ks[13], (DEPTH, CF_CH, D_MODEL), CF_CH ** -0.5),
        "w_out": nrm(ks[14], (DEPTH, D_MODEL, D_MODEL), D_MODEL ** -0.5),
        "ple_norm_g": 1.0 + nrm(ks[15], (DEPTH, D_MODEL), 0.02),
        "w_ple_gate": nrm(ks[16], (DEPTH, D_MODEL, D_MODEL), D_MODEL ** -0.5),
        "w_ple_proj": nrm(ks[17], (DEPTH, PLE_DIM, D_MODEL), PLE_DIM ** -0.5),
        "final_norm_g": 1.0 + nrm(ks[18], (D_MODEL,), 0.02),
    }


def reference(x, p, mix_norm_g, w_in, dn_conv_w, dn_a_log, dn_dt_bias, dn_out_norm_g, w_dn_out,
              cf_dw_w, cf_dw_b, cf_ln_g, cf_ln_b, w_cf_out, w_out, ple_norm_g, w_ple_gate,
              w_ple_proj, final_norm_g):
    b, t = x.shape[0], x.shape[1]
    split_points = np.cumsum(np.array(IN_SPLITS))[:-1].tolist()
    for i in range(DEPTH):
        h = _rmsnorm(x, mix_norm_g[i])
        proj = jnp.einsum("btd,dn->btn", h, w_in[i])
        qkv, z_dn, beta_l, decay_l, glu, z_cf, gate_l = jnp.split(proj, split_points, axis=-1)

        qkv = jax.nn.silu(_causal_dwconv(qkv, dn_conv_w[i]))
        q, k, v = jnp.split(qkv, [DN_QK, 2 * DN_QK], axis=-1)
        q = _l2norm(q.reshape(b, t, DN_HEADS, DN_HEAD_K))
        k = _l2norm(k.reshape(b, t, DN_HEADS, DN_HEAD_K))
        v = v.reshape(b, t, DN_HEADS, DN_HEAD_V)
        beta = jax.nn.sigmoid(beta_l.astype(jnp.float32))
        log_decay = -jnp.exp(dn_a_log[i].astype(jnp.float32)) * jax.nn.softplus(
            decay_l.astype(jnp.float32) + dn_dt_bias[i].astype(jnp.float32))
        o = _chunk_gated_delta_rule(q, k, v, log_decay, beta)
        o = _rmsnorm(o, dn_out_norm_g[i]).astype(x.dtype) * jax.nn.silu(
            z_dn.reshape(b, t, DN_HEADS, DN_HEAD_V))
        u_dn = jnp.einsum("btc,cd->btd", o.reshape(b, t, DN_V), w_dn_out[i])

        c = glu[..., :CF_CH] * jax.nn.sigmoid(glu[..., CF_CH:])
        c = _causal_dwconv(c, cf_dw_w[i]) + cf_dw_b[i]
        c = jax.nn.silu(_layernorm(c, cf_ln_g[i], cf_ln_b[i])) * jax.nn.silu(z_cf)
        u_cf = jnp.einsum("btc,cd->btd", c, w_cf_out[i])

        g_dn, g_cf = jnp.split(jax.nn.sigmoid(gate_l), N_BRANCH, axis=-1)
        x = x + jnp.einsum("btd,de->bte", g_dn * u_dn + g_cf * u_cf, w_out[i])

        e = jnp.einsum("btp,pd->btd", p[i], w_ple_proj[i])
        gate = jax.nn.sigmoid(jnp.einsum("btd,de->bte", _rmsnorm(x, ple_norm_g[i]), w_ple_gate[i]))
        x = x + gate * e
    return _rmsnorm(x, final_norm_g)
```

```python
import contextlib
import math
import numpy as np
import ml_dtypes
import concourse.bass as bass
import concourse.mybir as mybir
from concourse.bass_utils import run_bass_kernel_spmd

F32 = mybir.dt.float32
BF16 = mybir.dt.bfloat16
AF = mybir.ActivationFunctionType
ALU = mybir.AluOpType

NSLOT = 32
T_CORE = 4096
D = 1024
TT = 512
NIN = 9232
EPS = 1e-6


class Buf:
    __slots__ = ("name", "w", "r")

    def __init__(self, name):
        self.name = name
        self.w = {}
        self.r = {}


class Op:
    __slots__ = ("eng", "fn", "reads", "writes", "dma", "eidx", "waits", "signal", "semkey", "sigval", "gidx", "event")


def _later(a, b):
    return a if a.gidx > b.gidx else b


class Prog:
    ENGS = ("pe", "act", "dve", "pool", "sp")

    def __init__(self, nc):
        self.nc = nc
        self.ops = []
        self.per_eng = {e: [] for e in self.ENGS}
        self.ndma = 0
        self.arena = Buf("ARENA")

    def op(self, eng, fn, reads=(), writes=(), dma=False):
        o = Op()
        o.eng = eng
        o.fn = fn
        o.reads = [b for b in reads if b is not None]
        o.writes = [b for b in writes if b is not None]
        o.dma = dma
        o.waits = []
        o.signal = dma
        o.semkey = None
        o.sigval = None
        o.event = None
        o.gidx = len(self.ops)
        o.eidx = len(self.per_eng[eng])
        if dma:
            k = self.ndma
            self.ndma += 1
            o.semkey = ("slot", k % NSLOT)
            o.sigval = 16 * (k // NSLOT + 1)
        self.ops.append(o)
        self.per_eng[eng].append(o)
        return o

    def event(self, kind, bufs):
        o = Op()
        o.eng = None
        o.event = (kind, list(bufs))
        o.gidx = len(self.ops)
        self.ops.append(o)

    def dma(self, eng, out, in_, reads=(), writes=()):
        return self.op(eng, lambda e: e.dma_start(out=out, in_=in_), reads, writes, dma=True)

    def _key(self, o):
        return o.semkey if o.dma else o.eng

    def analyze(self):
        A = self.arena
        for o in self.ops:
            if o.event is not None:
                kind, bufs = o.event
                if kind == "alias":
                    new_b, old_b = bufs
                    for b in new_b:
                        for ob in old_b:
                            for k, d in ob.w.items():
                                b.w[k] = _later(b.w[k], d) if k in b.w else d
                            for k, d in ob.r.items():
                                b.r[k] = _later(b.r[k], d) if k in b.r else d
                elif kind == "free":
                    for b in bufs:
                        for k, d in b.w.items():
                            A.w[k] = _later(A.w[k], d) if k in A.w else d
                        for k, d in b.r.items():
                            A.r[k] = _later(A.r[k], d) if k in A.r else d
                else:
                    for b in bufs:
                        b.w = dict(A.w)
                        b.r = dict(A.r)
                continue
            raw = set()
            other = set()
            for b in o.reads:
                raw.update(b.w.values())
            for b in o.writes:
                other.update(b.w.values())
                other.update(b.r.values())
            other -= raw
            need = []
            for d, is_raw in [(d, True) for d in raw] + [(d, False) for d in other]:
                if d is o:
                    continue
                if d.dma:
                    need.append(d)
                elif d.eng == o.eng and not o.dma:
                    if is_raw and o.eng != "pe" and (o.eidx - d.eidx) <= 3:
                        need.append(d)
                else:
                    need.append(d)
            for d in need:
                d.signal = True
            o.waits = need
            k = self._key(o)
            for b in o.reads:
                b.r[k] = o
            for b in o.writes:
                b.w = {k: o}
                b.r = {}
        cnt = {e: 0 for e in self.ENGS}
        for o in self.ops:
            if o.event is None and not o.dma and o.signal:
                cnt[o.eng] += 1
                o.semkey = ("eng", o.eng)
                o.sigval = cnt[o.eng]

    def emit(self, final_wait_ops=()):
        nc = self.nc
        self.analyze()
        with contextlib.ExitStack() as st:
            sems = {}
            for e in self.ENGS:
                sems[("eng", e)] = st.enter_context(nc.semaphore("s_" + e))
            for j in range(NSLOT):
                sems[("slot", j)] = st.enter_context(nc.semaphore("d_%d" % j))
            block = st.enter_context(nc.Block())

            def run(engname, eng):
                seen = {}
                for o in self.per_eng[engname]:
                    wl = {}
                    for d in o.waits:
                        wl[d.semkey] = max(wl.get(d.semkey, 0), d.sigval)
                    if o.dma and o.sigval > 16:
                        wl[o.semkey] = max(wl.get(o.semkey, 0), o.sigval - 16)
                    for key, val in wl.items():
                        if seen.get(key, 0) >= val:
                            continue
                        eng.wait_ge(sems[key], val)
                        seen[key] = val
                    ins = o.fn(eng)
                    if o.signal:
                        ins.then_inc(sems[o.semkey], 16 if o.dma else 1)
                if engname == "sp":
                    for o in final_wait_ops:
                        if seen.get(o.semkey, 0) < o.sigval:
                            eng.wait_ge(sems[o.semkey], o.sigval)
                            seen[o.semkey] = o.sigval

            @block.sync
            def _(e):
                run("sp", e)

            @block.tensor
            def _(e):
                run("pe", e)

            @block.scalar
            def _(e):
                run("act", e)

            @block.vector
            def _(e):
                run("dve", e)

            @block.gpsimd
            def _(e):
                run("pool", e)


def build_program(n_tiles=T_CORE // TT, debug=False):
    nc = bass.Bass("TRN2", target_bir_lowering=False)
    P = Prog(nc)
    Tn = n_tiles * TT

    def din(name, shape, dt=F32):
        return nc.dram_tensor(name, shape, dt, kind="ExternalInput").ap()

    x_d = din("x", [Tn, D])
    p_d = din("p", [Tn, 256])
    w_in_d = din("w_in", [D, NIN])
    w_dn_d = din("w_dn_out", [D, D])
    w_cf_d = din("w_cf_out", [D, D])
    w_out_d = din("w_out", [D, D])
    w_pg_d = din("w_ple_gate", [D, D])
    w_pp_d = din("w_ple_proj", [256, D])
    mng_d = din("mix_norm_g", [128, 8])
    png_d = din("ple_norm_g", [128, 8])
    dng_d = din("dn_out_norm_g", [128, 1])
    cw_d = din("dn_conv_w", [128, 24, 4])
    cfw_d = din("cf_dw_w", [128, 8, 31])
    cfb_d = din("cf_dw_b", [128, 8])
    lng_d = din("cf_ln_g", [128, 8])
    lnb_d = din("cf_ln_b", [128, 8])
    alog_d = din("dn_a_log", [128, 8])
    dtb_d = din("dn_dt_bias", [128, 8])
    fg_d = din("final_norm_g", [128, D])
    c_ident_d = din("c_ident", [128, 128], BF16)
    c_ones_d = din("c_ones", [128, 128], BF16)
    c_onesf_d = din("c_onesf", [128, 128])
    c_ltri_d = din("c_ltri", [128, 128])
    c_ltrib_d = din("c_ltrib", [128, 128], BF16)
    c_ugt_d = din("c_ugt", [128, 8, 128])
    c_negm_d = din("c_negm", [128, 512], BF16)
    c_strict_d = din("c_strict", [128, 1024], BF16)
    c_identt_d = din("c_identt", [128, 1024], BF16)
    c_esel_d = din("c_esel", [128, 16, 64], BF16)
    c_bsel_d = din("c_bsel", [64, 16, 128], BF16)
    y_d = nc.dram_tensor("y", [Tn, D], F32, kind="ExternalOutput").ap()

    dbg = {}
    if debug:
        for nm_, shp_, dt_ in [("dbg_hT", [128, 8, TT], BF16), ("dbg_qkv", [128, 24, 4 + TT], BF16),
                               ("dbg_sz", [128, 8, TT], BF16), ("dbg_og", [128, 8, TT], BF16),
                               ("dbg_cg", [128, 8, TT], BF16), ("dbg_m", [128, 8, TT], BF16),
                               ("dbg_x1", [128, 4, D], F32), ("dbg_cv", [128, 8, TT], BF16),
                               ("dbg_sm", [128, 16, 8], F32), ("dbg_T2T", [128, 1024], BF16),
                               ("dbg_DT", [128, 1024], BF16), ("dbg_of", [128, 8, 128], F32),
                               ("dbg_X0", [128, 1024], BF16), ("dbg_vn", [128, 8, 128], BF16)]:
            dbg[nm_] = nc.dram_tensor(nm_, shp_, dt_, kind="ExternalOutput").ap()

    def DBG(name, ap, reads, t=0, cond=True):
        if debug and cond and t == 0:
            out_dmas.append(P.dma("sp", dbg[name], ap, reads=reads))

    out_dmas = []

    def dscr(name, shape, dt=BF16):
        return nc.dram_tensor(name, shape, dt, kind="Internal").ap()

    winb = dscr("winb", [18, 128, 8, 512])
    wbdb = dscr("wbdb", [128, 8, 16])
    wdnb = dscr("wdnb", [2, 128, 8, 512])
    wcfb = dscr("wcfb", [2, 128, 8, 512])
    woutb = dscr("woutb", [2, 128, 8, 512])
    wpgb = dscr("wpgb", [2, 128, 8, 512])
    wppb = dscr("wppb", [2, 128, 2, 512])
    dgs = dscr("dgs", [8, 128, 31, 128])
    B_winb = [Buf("winb%d" % i) for i in range(18)]
    B_wbdb = Buf("wbdb")
    B_w2 = {n: [Buf(n + "0"), Buf(n + "1")] for n in ("dn", "cf", "out", "pg", "pp")}
    B_dgs = [Buf("dgs%d" % i) for i in range(8)]

    pst = contextlib.ExitStack()

    def sbp(name, shape, dt):
        return pst.enter_context(nc.sbuf_tensor(name, shape, dt))

    uid = [0]

    class Phase:
        def __init__(self):
            self.st = contextlib.ExitStack()
            self.bufs = []

        def sb(self, name, shape, dt):
            uid[0] += 1
            return self.st.enter_context(nc.sbuf_tensor("%s_%d" % (name, uid[0]), shape, dt))

        def buf(self, name):
            b = Buf(name)
            self.bufs.append(b)
            return b

        def bufs_n(self, name, n):
            return [self.buf("%s%d" % (name, i)) for i in range(n)]

        def begin(self):
            P.event("alloc", self.bufs)

        def end(self):
            P.event("free", self.bufs)
            self.st.close()

    def ACT(out, in_, func, reads, writes, **kw):
        return P.op("act", lambda e: e.activation(out=out, in_=in_, func=func, **kw), reads, writes)

    def TTo(eng, out, in0, in1, op, reads, writes):
        return P.op(eng, lambda e: e.tensor_tensor(out=out, in0=in0, in1=in1, op=op), reads, writes)

    def TS(eng, out, in0, s1, s2, op0, op1, reads, writes):
        if op1 is None:
            return P.op(eng, lambda e: e.tensor_scalar(out=out, in0=in0, scalar1=s1, scalar2=None, op0=op0), reads, writes)
        return P.op(eng, lambda e: e.tensor_scalar(out=out, in0=in0, scalar1=s1, scalar2=s2, op0=op0, op1=op1), reads, writes)

    def STT(out, in0, scalar, in1, op0, op1, reads, writes):
        return P.op("dve", lambda e: e.scalar_tensor_tensor(out=out, in0=in0, scalar=scalar, in1=in1, op0=op0, op1=op1), reads, writes)

    def MM(out, lhsT, rhs, start, stop, reads, writes):
        return P.op("pe", lambda e: e.matmul(out, lhsT=lhsT, rhs=rhs, start=start, stop=stop), reads, writes)

    def CP(eng, out, in_, reads, writes):
        if eng == "act":
            return P.op("act", lambda e: e.activation(out=out, in_=in_, func=AF.Copy), reads, writes)
        return P.op(eng, lambda e: e.tensor_copy(out=out, in_=in_), reads, writes)

    def MEMSET(eng, ap, val, writes):
        return P.op(eng, lambda e: e.memset(ap, val), (), writes)

    def bc8(t, n):
        return bass.AP(t, 0, [[8, 128], [1, 8], [0, n]])

    ident = sbp("ident", [128, 128], BF16)
    ones_bf = sbp("ones_bf", [128, 128], BF16)
    ones_f = sbp("ones_f", [128, 128], F32)
    ltri = sbp("ltri", [128, 128], F32)
    ltrib = sbp("ltrib", [128, 128], BF16)
    ugt = sbp("ugt", [128, 8, 128], F32)
    negm = sbp("negm", [128, 512], BF16)
    strictm = sbp("strictm", [128, 1024], BF16)
    identt = sbp("identt", [128, 1024], BF16)
    esel = sbp("esel", [128, 16, 64], BF16)
    bsel = sbp("bsel", [64, 16, 128], BF16)
    fg = sbp("fg", [128, D], F32)
    mng = sbp("mng", [128, 8], F32)
    png = sbp("png", [128, 8], F32)
    dng = sbp("dng", [128, 1], F32)
    cw = sbp("cw", [128, 24, 4], F32)
    cfw = sbp("cfw", [128, 8, 31], F32)
    cfb = sbp("cfb", [128, 8], F32)
    lng = sbp("lng", [128, 8], F32)
    lnb = sbp("lnb", [128, 8], F32)
    alog = sbp("alog", [128, 8], F32)
    dtb = sbp("dtb", [128, 8], F32)
    negA = sbp("negA", [128, 8], F32)
    wbd = sbp("wbd", [128, 8, 16], BF16)
    B_c = Buf("consts")
    B_negA = Buf("negA")
    B_wbd = Buf("wbd")
    B_cl = []
    for t_, d_ in [(ident, c_ident_d), (ones_bf, c_ones_d), (ones_f, c_onesf_d), (ltri, c_ltri_d), (ltrib, c_ltrib_d), (ugt, c_ugt_d),
                   (negm, c_negm_d), (strictm, c_strict_d), (identt, c_identt_d), (esel, c_esel_d), (bsel, c_bsel_d),
                   (fg, fg_d), (mng, mng_d), (png, png_d), (dng, dng_d), (cw, cw_d), (cfw, cfw_d), (cfb, cfb_d),
                   (lng, lng_d), (lnb, lnb_d), (alog, alog_d), (dtb, dtb_d)]:
        bb_ = Buf("c%d" % len(B_cl))
        B_cl.append(bb_)
        P.dma("sp", t_[:], d_, writes=[bb_])
    cdummy = sbp("cdummy", [128, 8], F32)
    P.op("pool", lambda e: e.memset(cdummy[:], 0.0), B_cl, [B_c])
    ACT(negA[:], alog[:], AF.Exp, [B_c], [B_negA])
    TS("dve", negA[:], negA[:], -1.0, None, ALU.mult, None, [B_negA], [B_negA])

    hTs = [sbp("hT%d" % i, [128, 8, TT], BF16) for i in range(2)]
    B_hTs = [[Buf("hT%d_%d" % (j, i)) for i in range(4)] for j in range(2)]
    st4h = sbp("st4h", [128, 8], F32)
    B_st4h = Buf("st4h")
    st4p = sbp("st4p", [128, 8], F32)
    B_stp = [Buf("st4p%d" % i) for i in range(4)]
    NRING = 3
    wring = [sbp("wring%d" % i, [128, 8, 512], BF16) for i in range(NRING)]
    B_wr = [Buf("wr%d" % i) for i in range(NRING)]
    S_f = sbp("S_f", [128, 8, 128], F32)
    S_b = sbp("S_b", [128, 8, 128], BF16)
    B_Sf = Buf("Sf")
    B_Sb = Buf("Sb")
    hist = sbp("hist", [128, 24, 4], BF16)
    B_hist = [Buf("hist%d" % i) for i in range(24)]
    cbuf = sbp("cbuf", [128, 8, 32 + TT], BF16)
    B_cb = [Buf("cb%d" % i) for i in range(8)]
    ogT = sbp("ogT", [128, 8, TT], BF16)
    B_og = [Buf("og%d" % i) for i in range(4)]
    st4 = sbp("st4", [128, 8], F32)
    B_st4 = Buf("st4")

    ps = [pst.enter_context(nc.psum_tensor("ps%d" % i, [128, 512], F32)) for i in range(8)]
    psb = [t_.bitcast(BF16) for t_ in ps]
    B_ps = [Buf("ps%d" % i) for i in range(8)]
    bank_ctr = [0]

    def bank():
        i = bank_ctr[0] % 7
        bank_ctr[0] += 1
        return i

    MEMSET("pool", S_f[:], 0.0, [B_Sf])
    MEMSET("pool", S_b[:], 0.0, [B_Sb])
    MEMSET("pool", hist[:], 0.0, B_hist)
    MEMSET("pool", cbuf[:], 0.0, B_cb)

    def head_alloc(tt_):
        phH_ = Phase()
        H = {"ph": phH_, "t": tt_}
        H["xtm"] = phH_.sb("xtmh", [128, 4, D], F32)
        H["B_x"] = phH_.bufs_n("xh", 4)
        H["hb"] = [phH_.sb("hbh%d" % i, [128, D], BF16) for i in range(4)]
        H["B_hb"] = phH_.bufs_n("hbh", 4)
        H["junk"] = H["hb"][3]
        H["B_junk"] = H["B_hb"][3]
        phH_.begin()
        rr = tt_ * TT
        for cc in range(4):
            P.dma("sp", H["xtm"][:, cc, :], x_d[rr + cc * 128:rr + (cc + 1) * 128, :], writes=[H["B_x"][cc]])
        return H

    def head_act(H):
        xtm_, B_x_, hb_, B_hb_, junk_, B_junk_ = H["xtm"], H["B_x"], H["hb"], H["B_hb"], H["junk"], H["B_junk"]
        for cc in range(4):
            ACT(junk_[:], xtm_[:, cc, :], AF.Square, [B_x_[cc]], [B_junk_, B_st4h], accum_out=st4h[:, cc:cc + 1])
        ACT(st4h[:, 4:8], st4h[:, 0:4], AF.Ln, [B_st4h], [B_st4h], scale=1.0 / D, bias=EPS)
        ACT(st4h[:, 4:8], st4h[:, 4:8], AF.Exp, [B_st4h], [B_st4h], scale=-0.5)
        for cc in range(4):
            ACT(hb_[cc][:], xtm_[:, cc, :], AF.Copy, [B_x_[cc], B_st4h], [B_hb_[cc]], scale=st4h[:, 4 + cc:5 + cc])

    def head_pe(H):
        hb_, B_hb_ = H["hb"], H["B_hb"]
        hT_, B_hT_ = hTs[H["t"] % 2], B_hTs[H["t"] % 2]
        for cc in range(4):
            b = bank()
            for k in range(8):
                P.op("pe", lambda e, o_=psb[b][:, k * 128:(k + 1) * 128], i_=hb_[cc][:, k * 128:(k + 1) * 128]:
                     e.transpose(out=o_, in_=i_, identity=ident[:]), [B_hb_[cc], B_c], [B_ps[b]])
            TTo("dve", hT_[:, :, cc * 128:(cc + 1) * 128], psb[b][:, 0:1024].rearrange("p (k t) -> p k t", k=8),
                bass.AP(mng, 0, [[8, 128], [1, 8], [0, 128]]), ALU.mult, [B_ps[b], B_c], [B_hT_[cc]])
        H["ph"].end()

    def head_compute(H):
        head_act(H)
        head_pe(H)


    head_compute(head_alloc(0))

    dcs = dscr("dcs", [3, 128, 8, 512])
    B_dcs = [Buf("dcs%d" % i) for i in range(3)]
    ph = Phase()
    prep_lo = nc.sbuf_base
    dgt = [ph.sb("dgt%d" % i, [128, 32, 128], BF16) for i in range(2)]
    B_dgt = ph.bufs_n("dgt", 2)
    prep_hi = nc.sbuf_base
    ph.begin()
    w_in_v = w_in_d.rearrange("(k p) n -> p k n", p=128)
    P.dma("pool", wbd[:], w_in_v[:, :, 4096:4112], writes=[B_wbd])
    for g in range(3):
        a_ = g % 2
        TTo("dve", dgt[a_][:], bass.AP(ident, 0, [[128, 128], [0, 32], [1, 128]]),
            bass.AP(cw, g * 32, [[96, 128], [1, 32], [0, 128]]), ALU.mult, [B_c], [B_dgt[a_]])
        P.dma("sp", dcs[g].rearrange("p k (a m) -> p (k a) m", m=128), dgt[a_][:], reads=[B_dgt[a_]], writes=[B_dcs[g]])
    for c in range(8):
        a_ = (c + 1) % 2
        TTo("dve", dgt[a_][:, 0:31, :], bass.AP(ident, 0, [[128, 128], [0, 31], [1, 128]]),
            bass.AP(cfw, c * 31, [[248, 128], [1, 31], [0, 128]]), ALU.mult, [B_c], [B_dgt[a_]])
        P.dma("sp", dgs[c], dgt[a_][:, 0:31, :], reads=[B_dgt[a_]], writes=[B_dgs[c]])
    ph.st.close()

    order = []
    for g in range(8):
        order.append(("in", g))
    order += [("dc", 0), ("dc", 1), ("dc", 2)]
    for g in (10, 11, 8, 9, 12, 13):
        order.append(("in", g))
    order += [("dg", c) for c in range(8)]
    for g in (14, 15, 16, 17):
        order.append(("in", g))
    for nm in ("dn", "cf", "out"):
        order += [(nm, 0), (nm, 1)]
    order += [("pg", 0), ("pp", 0), ("pg", 1), ("pp", 1)]
    NG = len(order)
    wstate = {"issued": 0, "used": 0}

    w2v = {"dn": w_dn_d.rearrange("(k p) n -> p k n", p=128), "cf": w_cf_d.rearrange("(k p) n -> p k n", p=128),
           "out": w_out_d.rearrange("(k p) n -> p k n", p=128), "pg": w_pg_d.rearrange("(k p) n -> p k n", p=128),
           "pp": w_pp_d.rearrange("(k p) n -> p k n", p=128)}
    scr2 = {"dn": wdnb, "cf": wcfb, "out": woutb, "pg": wpgb, "pp": wppb}

    pending_store = {}

    def w_issue():
        i = wstate["issued"]
        if i >= NG * n_tiles:
            return
        nm, g = order[i % NG]
        slot = i % NRING
        first = i < NG
        dst = wring[slot][:, 0:2, :] if nm == "pp" else wring[slot][:]
        pend = pending_store.pop(slot, None)
        if pend is not None:
            P.dma("pool", pend[0], pend[1], reads=[B_wr[slot]], writes=[pend[2]])
        if nm == "dc":
            P.dma("sp", dst, dcs[g], reads=[B_dcs[g]], writes=[B_wr[slot]])
        elif nm == "dg":
            P.dma("sp", wring[slot][:].rearrange("p k n -> p (k n)")[:, 0:31 * 128], dgs[g].rearrange("p j m -> p (j m)"),
                  reads=[B_dgs[g]], writes=[B_wr[slot]])
        elif nm == "in":
            if first:
                s0 = g * 512 if g < 8 else 4112 + (g - 8) * 512
                P.dma("pool", dst, w_in_v[:, :, s0:s0 + 512], writes=[B_wr[slot]])
                if n_tiles > 1:
                    pending_store[slot] = (winb[g], dst, B_winb[g])
            else:
                P.dma("sp", dst, winb[g], reads=[B_winb[g]], writes=[B_wr[slot]])
        else:
            if first:
                P.dma("pool", dst, w2v[nm][:, :, g * 512:(g + 1) * 512], writes=[B_wr[slot]])
                if n_tiles > 1:
                    pending_store[slot] = (scr2[nm][g], dst, B_w2[nm][g])
            else:
                P.dma("sp", dst, scr2[nm][g], reads=[B_w2[nm][g]], writes=[B_wr[slot]])
        wstate["issued"] += 1

    def w_get(expect, hold=0):
        i = wstate["used"]
        assert order[i % NG] == expect, (order[i % NG], expect)
        while wstate["issued"] < min(i + NRING - hold, NG * n_tiles):
            w_issue()
        wstate["used"] += 1
        slot = i % NRING
        return wring[slot], B_wr[slot]

    def proj_fm(expect, rhs_t, rhs_bufs, evac):
        wt, wb = w_get(expect)
        for j in range(4):
            b = bank()
            for k in range(8):
                MM(ps[b][:, :], wt[:, k, j * 128:(j + 1) * 128], rhs_t[:, k, :], k == 0, k == 7,
                   [wb] + rhs_bufs, [B_ps[b]])
            evac(j, b)

    def proj_units(expect, rhs_t, rhs_bufs, evac):
        wt, wb = w_get(expect)
        for j in range(4):
            b = bank()
            for k in range(8):
                MM(ps[b][:, :], wt[:, k, j * 128:(j + 1) * 128], rhs_t[:, k, :], k == 0, k == 7,
                   [wb] + rhs_bufs, [B_ps[b]])
            evac(j, b)
            yield

    LNQ = math.log(128.0 ** -0.5)
    for t in range(n_tiles):
        r0 = t * TT
        hT, B_hT = hTs[t % 2], B_hTs[t % 2]
        DBG('dbg_hT', hT[:], B_hT, t)
        phc_lo = nc.sbuf_base
        phC = Phase()
        sgb = phC.sb("sgbcv", [128, 8, TT], BF16)
        cvb = sgb
        B_sgb = phC.bufs_n("sgbcv", 8)
        B_cv = B_sgb
        szc = phC.sb("szc", [128, 8, TT], BF16)
        B_szc = phC.bufs_n("szc", 8)
        if t == 0:
            assert prep_lo == phc_lo and nc.sbuf_base >= prep_hi, (prep_lo, phc_lo, prep_hi, nc.sbuf_base)
        phC.begin()
        if t == 0:
            P.event("alias", (phC.bufs, B_dgt))
        ph = Phase()
        qkvr = ph.sb("qkvr", [128, 24, 4 + TT], BF16)
        B_q = ph.bufs_n("q", 24)
        sz = ph.sb("sz", [128, 8, TT], BF16)
        B_sz = ph.bufs_n("sz", 8)
        def mkset(si):
            S = {}
            S["on"] = ph.sb("on%d" % si, [128, 1024], BF16)
            S["B_on"] = ph.buf("on")
            S["ldU"] = ph.sb("ldU%d" % si, [128, 8, 128], BF16)
            S["B_ldU"] = ph.buf("ldU")
            for nm_ in ("DT", "DTsb", "QKm"):
                S[nm_] = ph.sb("%s%d" % (nm_, si), [128, 1024], BF16)
                S["B_" + nm_] = ph.buf(nm_)
            S["DTs"], S["B_DTs"] = S["DTsb"], S["B_DTsb"]
            S["Yo"] = ph.sb("Yo%d" % si, [128, 8, 64], BF16)
            S["B_Yo"] = ph.buf("Yo")
            for nm_ in ("GK", "Kd", "Vt", "nww"):
                S[nm_] = ph.sb("%s%d" % (nm_, si), [128, 8, 128], BF16)
                S["B_" + nm_] = ph.buf(nm_)
            for nm_ in ("X", "Y", "W"):
                S[nm_ + "s"] = [ph.sb("%s%d_%d" % (nm_, si, i), [128, 1024], BF16) for i in range(2)]
                S["B_" + nm_] = [ph.bufs_n(nm_ + "a", 2), ph.bufs_n(nm_ + "b", 2)]
            return S

        SETS = [mkset(0), mkset(1)]
        vn = ph.sb("vn", [128, 8, 128], BF16)
        B_vn = ph.buf("vn")
        of = ph.sb("of", [128, 8, 128], F32)
        B_of = ph.buf("of")
        sqt = [SETS[i]["on"][:, 0:TT] for i in range(2)]
        B_sqt = [SETS[i]["B_on"] for i in range(2)]
        lnv = of[0:64, 0:4, :].rearrange("p h c -> p (h c)")
        B_lnv = B_of
        rsb = vn[0:64, 0:4, :].rearrange("p h c -> p (h c)")
        B_rsb = B_vn
        smc = [ph.sb("smc%d" % i, [128, 16, 8], F32) for i in range(4)]
        B_smc = ph.bufs_n("smc", 4)
        gsbc = [ph.sb("gsbc%d" % i, [128, 16], F32) for i in range(4)]
        ph.begin()
        BET, NEGB, TMPD, EE, SP, LD, GS, EGL, DGL, EGD, SSQ, RSO = range(12)

        for g in range(6):
            def ev(j, b, g=g):
                c = g * 4 + j
                CP("act" if (c % 2 == 0 or t == 0) else "dve", qkvr[:, c, 4:4 + TT], ps[b][:, :], [B_ps[b]], [B_q[c]])
            proj_fm(("in", g), hT, B_hT, ev)
        for g in (6, 7):
            def ev(j, b, g=g):
                c = (g - 6) * 4 + j
                ACT(sz[:, c, :], ps[b][:, :], AF.Silu, [B_ps[b]], [B_sz[c]])
            proj_fm(("in", g), hT, B_hT, ev)
        def bd_chain(cc):
            hs_ = slice(cc * 128, (cc + 1) * 128)
            sm, gsb, B_sm = smc[cc], gsbc[cc], B_smc[cc]

            def smv(i):
                return sm[:, i, :]

            bbd = bank()
            for k in range(8):
                MM(ps[bbd][:, 0:16], hT[:, k, hs_], wbd[:, k, :], k == 0, k == 7, B_hT + [B_wbd], [B_ps[bbd]])
            ACT(smv(BET), ps[bbd][:, 0:8], AF.Exp, [B_ps[bbd]], [B_sm], scale=-1.0)
            TS("dve", smv(BET), smv(BET), 1.0, None, ALU.add, None, [B_sm], [B_sm])
            P.op("dve", lambda e, o_=smv(BET): e.reciprocal(out=o_, in_=o_), [B_sm], [B_sm])
            TTo("dve", smv(TMPD), ps[bbd][:, 8:16], dtb[:], ALU.add, [B_ps[bbd], B_c], [B_sm])
            yield
            ACT(smv(EE), smv(TMPD), AF.Abs, [B_sm], [B_sm])
            ACT(smv(EE), smv(EE), AF.Exp, [B_sm], [B_sm], scale=-1.0)
            ACT(smv(SP), smv(EE), AF.Ln, [B_sm], [B_sm], bias=1.0)
            TS("dve", smv(DGL), smv(TMPD), 0.0, None, ALU.max, None, [B_sm], [B_sm])
            TTo("dve", smv(SP), smv(SP), smv(DGL), ALU.add, [B_sm], [B_sm])
            yield
            TTo("dve", smv(LD), smv(SP), negA[:], ALU.mult, [B_sm, B_negA], [B_sm])
            TS("dve", smv(NEGB), smv(BET), -1.0, None, ALU.mult, None, [B_sm], [B_sm])
            yield
            bg = bank()
            MM(ps[bg][:, 0:8], ltri[:], smv(LD), True, True, [B_sm, B_c], [B_ps[bg]])
            MM(ps[bg][:, 8:16], ones_f[:], smv(LD), True, True, [B_sm, B_c], [B_ps[bg]])
            CP("act", gsb[:], ps[bg][:, 0:16], [B_ps[bg]], [B_sm])
            yield
            ACT(smv(GS), gsb[:, 0:8], AF.Exp, [B_sm], [B_sm])
            ACT(smv(EGL), gsb[:, 8:16], AF.Exp, [B_sm], [B_sm])
            TTo("dve", smv(DGL), gsb[:, 8:16], gsb[:, 0:8], ALU.subtract, [B_sm], [B_sm])
            ACT(smv(EGD), smv(DGL), AF.Exp, [B_sm], [B_sm])
            yield

        _bdg = [bd_chain(c_) for c_ in range(4)]
        while _bdg:
            for g_ in list(_bdg):
                try:
                    next(g_)
                except StopIteration:
                    _bdg.remove(g_)
        for g in range(3):
            wt, wb = w_get(("dc", g))
            wflat = wt[:].rearrange("p k n -> p (k n)")
            for c8 in range(8):
                c = g * 8 + c8
                CP("pool", qkvr[:, c, 1:4], hist[:, c, 0:3], [B_hist[c]], [B_q[c]])
                b = bank()
                for j in range(4):
                    idx = c8 * 4 + j
                    MM(ps[b][:, :], wflat[:, idx * 128:(idx + 1) * 128], qkvr[:, c, 1 + j:1 + j + TT], j == 0, j == 3,
                       [wb, B_q[c]], [B_ps[b]])
                CP("pool", hist[:, c, 0:3], qkvr[:, c, 1 + TT:4 + TT], [B_q[c]], [B_hist[c]])
                ACT(qkvr[:, c, 4:4 + TT], ps[b][:, :], AF.Silu, [B_ps[b]], [B_q[c]])
        bss = bank()
        for i in range(16):
            a = i % 2
            ACT(sqt[a][:], qkvr[:, i, 4:4 + TT], AF.Square, [B_q[i]], [B_sqt[a]])
            MM(ps[bss][0:64, :], esel[:, i, :], sqt[a][:], i == 0, i == 15, [B_sqt[a], B_c], [B_ps[bss]])
        ACT(lnv[:, :], ps[bss][0:64, :], AF.Ln, [B_ps[bss]], [B_lnv], bias=EPS)
        MEMSET("pool", rsb[:], 0.0, [B_rsb])
        ACT(rsb[0:8, :], lnv[0:8, :], AF.Exp, [B_lnv], [B_rsb], scale=-0.5, bias=LNQ)
        ACT(rsb[32:40, :], lnv[32:40, :], AF.Exp, [B_lnv], [B_rsb], scale=-0.5)
        for i in range(16):
            b = bank()
            MM(ps[b][:, :], bsel[:, i, :], rsb[:, :], True, True, [B_rsb, B_c], [B_ps[b]])
            TTo("dve", qkvr[:, i, 4:4 + TT], qkvr[:, i, 4:4 + TT], ps[b][:, :], ALU.mult, [B_q[i], B_ps[b]], [B_q[i]])

        DBG('dbg_qkv', qkvr[:], B_q, t)
        DBG('dbg_sz', sz[:], B_sz, t)
        def chunk_A(cc, S):
            ts_ = slice(4 + cc * 128, 4 + (cc + 1) * 128)
            hs_ = slice(cc * 128, (cc + 1) * 128)
            sm, gsb, ldU, DT, DTs, DTsb, QKm = smc[cc], gsbc[cc], S["ldU"], S["DT"], S["DTs"], S["DTsb"], S["QKm"]
            on, B_on = S["on"], S["B_on"]
            GK, Kd, Vt, nww, Xs, Ys, Ws = S["GK"], S["Kd"], S["Vt"], S["nww"], S["Xs"], S["Ys"], S["Ws"]
            B_sm, B_ldU, B_DT, B_DTs, B_DTsb, B_QKm = B_smc[cc], S["B_ldU"], S["B_DT"], S["B_DTs"], S["B_DTsb"], S["B_QKm"]
            B_GK, B_Kd, B_Vt, B_nww, B_X, B_Y, B_W = S["B_GK"], S["B_Kd"], S["B_Vt"], S["B_nww"], S["B_X"], S["B_Y"], S["B_W"]
            T2T = Ws[1]
            B_T2T = B_W[1]
            Yo, B_Yo = S["Yo"], S["B_Yo"]

            def smv(i):
                return sm[:, i, :]

            TTo("dve", ldU[:], ugt[:], bass.AP(sm, LD * 8, [[128, 128], [1, 8], [0, 128]]), ALU.mult,
                [B_sm, B_c], [B_ldU])
            yield
            barg = [bank(), bank()]
            for i in range(2):
                MM(ps[barg[i]][:, :], ident[:], negm[:], True, False, [B_c], [B_ps[barg[i]]])
            for h in range(8):
                b = barg[h // 4]
                o_ = ps[b][:, (h % 4) * 128:(h % 4 + 1) * 128]
                MM(o_, ldU[:, h, :], ltrib[:], False, True, [B_ldU, B_c], [B_ps[b]])
            for i in range(2):
                ACT(DT[:, i * 512:(i + 1) * 512], ps[barg[i]][:, :], AF.Exp, [B_ps[barg[i]]], [B_DT])
            TTo("dve", DTs[:], DT[:], strictm[:], ALU.mult, [B_DT, B_c], [B_DTs])
            TTo("dve", DTsb[:].rearrange("p (h c) -> p h c", h=8), DTs[:].rearrange("p (h c) -> p h c", h=8),
                bass.AP(sm, NEGB * 8, [[128, 128], [1, 8], [0, 128]]), ALU.mult, [B_DTs, B_sm], [B_DTsb])
            yield
            bt = bank()
            for h in range(8):
                P.op("pe", lambda e, o_=psb[bt][:, h * 128:(h + 1) * 128], i_=qkvr[:, 8 + h, ts_]:
                     e.transpose(out=o_, in_=i_, identity=ident[:]), [B_q[8 + h], B_c], [B_ps[bt]])
            pv = psb[bt][:, 0:1024].rearrange("p (h c) -> p h c", h=8)
            TTo("dve", GK[:], pv, bass.AP(sm, GS * 8, [[128, 128], [1, 8], [0, 128]]), ALU.mult, [B_ps[bt], B_sm], [B_GK])
            TTo("dve", Kd[:], pv, bass.AP(sm, EGD * 8, [[128, 128], [1, 8], [0, 128]]), ALU.mult, [B_ps[bt], B_sm], [B_Kd])
            bt = bank()
            for h in range(8):
                P.op("pe", lambda e, o_=psb[bt][:, h * 128:(h + 1) * 128], i_=qkvr[:, 16 + h, ts_]:
                     e.transpose(out=o_, in_=i_, identity=ident[:]), [B_q[16 + h], B_c], [B_ps[bt]])
            CP("act", Vt[:], psb[bt][:, 0:1024].rearrange("p (h c) -> p h c", h=8), [B_ps[bt]], [B_Vt])
            yield
            bkk = [bank(), bank()]
            bqk = [bank(), bank()]
            for h in range(8):
                kT = qkvr[:, 8 + h, ts_]
                qT = qkvr[:, h, ts_]
                sl = slice((h % 4) * 128, (h % 4 + 1) * 128)
                MM(ps[bkk[h // 4]][:, sl], kT, kT, True, True, [B_q[8 + h]], [B_ps[bkk[h // 4]]])
                MM(ps[bqk[h // 4]][:, sl], kT, qT, True, True, [B_q[8 + h], B_q[h]], [B_ps[bqk[h // 4]]])
            for i in range(2):
                fs = slice(i * 512, (i + 1) * 512)
                TTo("dve", Xs[0][:, fs], ps[bkk[i]][:, :], DTsb[:, fs], ALU.mult, [B_ps[bkk[i]], B_DTsb], [B_X[0][i]])
                TTo("dve", QKm[:, fs], ps[bqk[i]][:, :], DT[:, fs], ALU.mult, [B_ps[bqk[i]], B_DT], [B_QKm])
            bt = bank()
            for h in range(8):
                P.op("pe", lambda e, o_=psb[bt][:, h * 128:(h + 1) * 128], i_=Xs[0][:, h * 128:(h + 1) * 128]:
                     e.transpose(out=o_, in_=i_, identity=ident[:]), [B_X[0][h // 4], B_c], [B_ps[bt]])
            CP("act", Ys[0][:], psb[bt][:, 0:1024], [B_ps[bt]], B_Y[0])
            CP("pool", Yo[64:128, :, :], Ys[0][64:128, :].rearrange("p (h c) -> p h c", h=8)[:, :, 0:64], B_Y[0], [B_Yo])
            MEMSET("pool", Ys[0][64:128, :].rearrange("p (h c) -> p h c", h=8)[:, :, 0:64], 0.0, B_Y[0])
            MEMSET("pool", Xs[0][0:64, :].rearrange("p (h c) -> p h c", h=8)[:, :, 64:128], 0.0, B_X[0])
            for i in range(2):
                fs = slice(i * 512, (i + 1) * 512)
                TTo("dve", Ws[0][:, fs], Xs[0][:, fs], identt[:, fs], ALU.add, [B_X[0][i], B_c], [B_W[0][i]])
            yield
            for lv in range(1, 6):
                pi, ci = (lv - 1) % 2, lv % 2
                by = [bank(), bank()]
                bx = [bank(), bank()] if lv < 5 else None
                for h in range(8):
                    sl = slice((h % 4) * 128, (h % 4 + 1) * 128)
                    hsl = slice(h * 128, (h + 1) * 128)
                    MM(ps[by[h // 4]][:, sl], Xs[pi][:, hsl], Ys[pi][:, hsl], True, True, [B_X[pi][h // 4], B_Y[pi][h // 4]],
                       [B_ps[by[h // 4]]])
                    if bx is not None:
                        MM(ps[bx[h // 4]][:, sl], Ys[pi][:, hsl], Xs[pi][:, hsl], True, True, [B_X[pi][h // 4], B_Y[pi][h // 4]],
                           [B_ps[bx[h // 4]]])
                for i in range(2):
                    fs = slice(i * 512, (i + 1) * 512)
                    CP("act", Ys[ci][:, fs], ps[by[i]][:, :], [B_ps[by[i]]], [B_Y[ci][i]])
                    if bx is not None:
                        CP("dve", Xs[ci][:, fs], ps[bx[i]][:, :], [B_ps[bx[i]]], [B_X[ci][i]])
                yield
                bw = [bank(), bank()]
                MM(ps[bw[1]][:, :], ident[:], Ws[pi][:, 512:1024], True, False, [B_W[pi][1], B_c], [B_ps[bw[1]]])
                for h in range(8):
                    sl = slice((h % 4) * 128, (h % 4 + 1) * 128)
                    hsl = slice(h * 128, (h + 1) * 128)
                    MM(ps[bw[h // 4]][:, sl], Ys[ci][:, hsl], Ws[pi][:, hsl], h < 4, True, [B_W[pi][h // 4], B_Y[ci][h // 4]],
                       [B_ps[bw[h // 4]]])
                TTo("dve", Ws[ci][:, 0:512], ps[bw[0]][:, :], Ws[pi][:, 0:512], ALU.add, [B_ps[bw[0]], B_W[pi][0]], [B_W[ci][0]])
                CP("act", Ws[ci][:, 512:1024], ps[bw[1]][:, :], [B_ps[bw[1]]], [B_W[ci][1]])
                yield
            Wd, B_Wd = Ws[1], B_W[1]
            bp = bank()
            bz = bank()
            for h in range(8):
                MM(ps[bp][0:64, h * 64:(h + 1) * 64], Yo[64:128, h, :], Wd[64:128, h * 128 + 64:(h + 1) * 128], True, True,
                   [B_Yo, B_Wd[h // 4]], [B_ps[bp]])
                P.op("pe", lambda e, o_=psb[bz][0:64, h * 64:(h + 1) * 64], i_=Wd[0:64, h * 128:h * 128 + 64]:
                     e.transpose(out=o_, in_=i_, identity=ident[0:64, 0:64]), [B_Wd[h // 4], B_c], [B_ps[bz]])
            P1sb = Xs[0][0:64, 0:512]
            Zsb = Xs[1][0:64, 0:512]
            CP("act", P1sb, ps[bp][0:64, :], [B_ps[bp]], [B_X[0][0]])
            CP("dve", Zsb, psb[bz][0:64, 0:512], [B_ps[bz]], [B_X[1][0]])
            yield
            bw12 = bank()
            for h in range(8):
                MM(ps[bw12][0:64, h * 64:(h + 1) * 64], Zsb[:, h * 64:(h + 1) * 64], P1sb[:, h * 64:(h + 1) * 64], True, True,
                   [B_X[0][0], B_X[1][0]], [B_ps[bw12]])
            CP("dve", Wd[0:64, :].rearrange("p (h c) -> p h c", h=8)[:, :, 64:128],
               ps[bw12][0:64, :].rearrange("p (h c) -> p h c", h=8), [B_ps[bw12]], B_Wd)
            DBG('dbg_T2T', T2T[:], B_T2T, t, cc == 0)
            DBG('dbg_DT', DT[:], [B_DT], t, cc == 0)
            DBG('dbg_sm', sm[:], [B_sm], t, cc == 0)
            yield
            bww = [bank(), bank()]
            for h in range(8):
                sl = slice((h % 4) * 128, (h % 4 + 1) * 128)
                MM(ps[bww[h // 4]][:, sl], GK[:, h, :], T2T[:, h * 128:(h + 1) * 128], True, True, [B_GK, B_T2T[h // 4]],
                   [B_ps[bww[h // 4]]])
            for i in range(2):
                ACT(nww[:, i * 4:(i + 1) * 4, :], ps[bww[i]][:, :].rearrange("p (h c) -> p h c", h=4), AF.Copy,
                    [B_ps[bww[i]]], [B_nww], scale=-1.0)
            yield

        def chunk_B(cc, S):
            ts_ = slice(4 + cc * 128, 4 + (cc + 1) * 128)
            hs_ = slice(cc * 128, (cc + 1) * 128)
            sm, gsb, ldU, DT, DTs, DTsb, QKm = smc[cc], gsbc[cc], S["ldU"], S["DT"], S["DTs"], S["DTsb"], S["QKm"]
            on, B_on = S["on"], S["B_on"]
            GK, Kd, Vt, nww, Xs, Ys, Ws = S["GK"], S["Kd"], S["Vt"], S["nww"], S["Xs"], S["Ys"], S["Ws"]
            B_sm, B_ldU, B_DT, B_DTs, B_DTsb, B_QKm = B_smc[cc], S["B_ldU"], S["B_DT"], S["B_DTs"], S["B_DTsb"], S["B_QKm"]
            B_GK, B_Kd, B_Vt, B_nww, B_X, B_Y, B_W = S["B_GK"], S["B_Kd"], S["B_Vt"], S["B_nww"], S["B_X"], S["B_Y"], S["B_W"]
            T2T = Ws[1]
            B_T2T = B_W[1]
            Yo, B_Yo = S["Yo"], S["B_Yo"]

            def smv(i):
                return sm[:, i, :]

            bv = [bank(), bank()]
            for h in range(8):
                sl = slice((h % 4) * 128, (h % 4 + 1) * 128)
                MM(ps[bv[h // 4]][:, sl], T2T[:, h * 128:(h + 1) * 128], Vt[:, h, :], True, False, [B_T2T[h // 4], B_Vt],
                   [B_ps[bv[h // 4]]])
                MM(ps[bv[h // 4]][:, sl], nww[:, h, :], S_b[:, h, :], False, True, [B_nww, B_Sb], [B_ps[bv[h // 4]]])
            for i in range(2):
                TTo("dve", vn[:, i * 4:(i + 1) * 4, :], ps[bv[i]][:, :].rearrange("p (h c) -> p h c", h=4),
                    bass.AP(sm, BET * 8 + i * 4, [[128, 128], [1, 4], [0, 128]]), ALU.mult, [B_ps[bv[i]], B_sm], [B_vn])
            yield
            bo1 = [bank(), bank()]
            bo2 = [bank(), bank()]
            for h in range(8):
                sl = slice((h % 4) * 128, (h % 4 + 1) * 128)
                MM(ps[bo1[h // 4]][:, sl], qkvr[:, h, ts_], S_b[:, h, :], True, True, [B_q[h], B_Sb], [B_ps[bo1[h // 4]]])
                MM(ps[bo2[h // 4]][:, sl], QKm[:, h * 128:(h + 1) * 128], vn[:, h, :], True, True, [B_QKm, B_vn],
                   [B_ps[bo2[h // 4]]])
            for i in range(2):
                TTo("dve", of[:, i * 4:(i + 1) * 4, :], ps[bo1[i]][:, :].rearrange("p (h c) -> p h c", h=4),
                    bass.AP(sm, GS * 8 + i * 4, [[128, 128], [1, 4], [0, 128]]), ALU.mult, [B_ps[bo1[i]], B_sm], [B_of])
            for i in range(2):
                TTo("dve", of[:, i * 4:(i + 1) * 4, :], of[:, i * 4:(i + 1) * 4, :],
                    ps[bo2[i]][:, :].rearrange("p (h c) -> p h c", h=4), ALU.add, [B_ps[bo2[i]], B_of], [B_of])
            DBG('dbg_of', of[:], [B_of], t, cc == 0)
            DBG('dbg_vn', vn[:], [B_vn], t, cc == 0)
            yield
            bs = [bank(), bank()]
            for h in range(8):
                sl = slice((h % 4) * 128, (h % 4 + 1) * 128)
                MM(ps[bs[h // 4]][:, sl], Kd[:, h, :], vn[:, h, :], True, True, [B_Kd, B_vn], [B_ps[bs[h // 4]]])
            TTo("dve", S_f[:], S_f[:], bass.AP(sm, EGL * 8, [[128, 128], [1, 8], [0, 128]]), ALU.mult, [B_Sf, B_sm],
                [B_Sf])
            for i in range(2):
                TTo("dve", S_f[:, i * 4:(i + 1) * 4, :], S_f[:, i * 4:(i + 1) * 4, :],
                    ps[bs[i]][:, :].rearrange("p (h c) -> p h c", h=4), ALU.add, [B_ps[bs[i]], B_Sf], [B_Sf])
            CP("act", S_b[:], S_f[:], [B_Sf], [B_Sb])
            yield
            ofl = of[:].rearrange("p h c -> p (h c)")
            ACT(on[:], ofl, AF.Square, [B_of], [B_on])
            P.op("dve", lambda e, o_=smv(SSQ), i_=on[:].rearrange("p (h c) -> p h c", h=8):
                 e.tensor_reduce(out=o_, in_=i_, axis=mybir.AxisListType.X, op=ALU.add), [B_on], [B_sm])
            ACT(smv(RSO), smv(SSQ), AF.Ln, [B_sm], [B_sm], scale=1.0 / 128, bias=EPS)
            ACT(smv(RSO), smv(RSO), AF.Exp, [B_sm], [B_sm], scale=-0.5)
            TTo("dve", on[:].rearrange("p (h c) -> p h c", h=8), of[:],
                bass.AP(sm, RSO * 8, [[128, 128], [1, 8], [0, 128]]), ALU.mult, [B_of, B_sm], [B_on])
            yield

        def chunk_T(cc, S):
            hs_ = slice(cc * 128, (cc + 1) * 128)
            on, B_on = S["on"], S["B_on"]
            bt = bank()
            for h in range(8):
                P.op("pe", lambda e, o_=psb[bt][:, h * 128:(h + 1) * 128], i_=on[:, h * 128:(h + 1) * 128]:
                     e.transpose(out=o_, in_=i_, identity=ident[:]), [B_on, B_c], [B_ps[bt]])
            STT(ogT[:, :, hs_], psb[bt][:, 0:1024].rearrange("p (h c) -> p h c", h=8), dng[:, 0:1], sz[:, :, hs_],
                ALU.mult, ALU.mult, [B_ps[bt], B_c] + B_sz, [B_og[cc]])
            yield

        def m2():
            for g in (10, 11):
                wt, wb = w_get(("in", g))
                for j in range(4):
                    c = (g - 10) * 4 + j
                    b = bank()
                    for k in range(8):
                        MM(ps[b][:, :], wt[:, k, j * 128:(j + 1) * 128], hT[:, k, :], k == 0, k == 7, [wb] + B_hT, [B_ps[b]])
                    ACT(sgb[:, c, :], ps[b][:, :], AF.Sigmoid, [B_ps[b]], [B_sgb[c]])
                    yield
            for g in (8, 9):
                wt, wb = w_get(("in", g))
                for j in range(4):
                    c = (g - 8) * 4 + j
                    b = bank()
                    for k in range(8):
                        MM(ps[b][:, :], wt[:, k, j * 128:(j + 1) * 128], hT[:, k, :], k == 0, k == 7, [wb] + B_hT, [B_ps[b]])
                    TTo("dve", cbuf[:, c, 32:32 + TT], ps[b][:, :], sgb[:, c, :], ALU.mult, [B_ps[b], B_sgb[c]], [B_cb[c]])
                    yield
            for g in (12, 13):
                wt, wb = w_get(("in", g))
                for j in range(4):
                    c = (g - 12) * 4 + j
                    b = bank()
                    for k in range(8):
                        MM(ps[b][:, :], wt[:, k, j * 128:(j + 1) * 128], hT[:, k, :], k == 0, k == 7, [wb] + B_hT, [B_ps[b]])
                    ACT(szc[:, c, :], ps[b][:, :], AF.Silu, [B_ps[b]], [B_szc[c]])
                    yield
            for c in range(8):
                wt, wb = w_get(("dg", c))
                wflat = wt[:].rearrange("p k n -> p (k n)")
                b = 7
                for j in range(31):
                    MM(ps[b][:, :], wflat[:, j * 128:(j + 1) * 128], cbuf[:, c, 2 + j:2 + j + TT], j == 0, j == 30,
                       [wb, B_cb[c]], [B_ps[b]])
                    if j % 8 == 7:
                        yield
                ACT(cvb[:, c, :], ps[b][:, :], AF.Identity, [B_ps[b], B_c], [B_cv[c]], bias=cfb[:, c:c + 1])
                CP("pool", cbuf[:, c, 2:32], cbuf[:, c, 2 + TT:32 + TT], [B_cb[c]], [B_cb[c]])
                yield

        def seq(*gens):
            for g_ in gens:
                yield from g_

        RATE = 56.0 / 60.0
        pace = [0.0, 0, 0]

        def run_with(bg, gens):
            gens = list(gens)
            while gens:
                for g_ in list(gens):
                    try:
                        next(g_)
                    except StopIteration:
                        gens.remove(g_)
                pace[1] += 1
                pace[0] += RATE
                while pace[0] >= 1.0:
                    pace[0] -= 1.0
                    try:
                        next(bg)
                        pace[2] += 1
                    except StopIteration:
                        pass

        gm = m2()
        gA = [chunk_A(c_, SETS[c_ % 2]) for c_ in range(4)]
        gB = [chunk_B(c_, SETS[c_ % 2]) for c_ in range(4)]
        gT = [chunk_T(c_, SETS[c_ % 2]) for c_ in range(4)]
        plan = [(gA[0], ()), (gA[1], ()),
                (gB[0], (gA[0], gA[1])), (gB[1], (gB[0],)),
                (gT[0], (gB[0],)), (gT[1], (gB[1],)),
                (gA[2], (gB[0],)), (gA[3], (gB[1],)),
                (gB[2], (gA[2], gB[1], gT[0])), (gB[3], (gA[3], gB[2], gT[1])),
                (gT[2], (gB[2],)), (gT[3], (gB[3],))]
        pending, active, done = list(plan), [], set()
        while pending or active:
            for item in list(pending):
                if all(id(a_) in done for a_ in item[1]):
                    active.append(item[0])
                    pending.remove(item)
            for g_ in list(active):
                try:
                    next(g_)
                except StopIteration:
                    active.remove(g_)
                    done.add(id(g_))
            pace[1] += 1
            pace[0] += RATE
            while pace[0] >= 1.0:
                pace[0] -= 1.0
                try:
                    next(gm)
                    pace[2] += 1
                except StopIteration:
                    pass
        for _ in gm:
            pace[2] += 1
        if t == 0:
            pass
        DBG('dbg_og', ogT[:], B_og, t)
        ph.end()

        ph = Phase()
        cgT = ph.sb("cgT", [128, 8, TT], BF16)
        B_cg = ph.bufs_n("cg", 8)
        sgd = ph.sb("sgd", [128, 8, TT], BF16)
        sgc = ph.sb("sgc", [128, 8, TT], BF16)
        B_sgd = ph.bufs_n("sgd", 8)
        B_sgc = ph.bufs_n("sgc", 8)
        mT = ph.sb("mT", [128, 8, TT], BF16)
        B_m = ph.bufs_n("m", 8)
        tmpm = [ph.sb("tmpm%d" % i, [128, TT], F32) for i in range(2)]
        B_tmpm = ph.bufs_n("tmpm", 2)
        pstg = ph.sb("pstg", [128, 4, 256], F32)
        B_pstg = ph.bufs_n("pstg", 4)
        pbf = ph.sb("pbf", [128, 256], BF16)
        B_pbf = ph.buf("pbf")
        pT = ph.sb("pT", [128, 2, TT], BF16)
        B_pT = ph.bufs_n("pT", 4)
        sgp = [ph.sb("sgp%d" % i, [128, 512], F32) for i in range(2)]
        B_sgp = ph.bufs_n("sgp", 2)
        tmpe = [ph.sb("tmpe%d" % i, [128, 512], F32) for i in range(2)]
        B_tmpe = ph.bufs_n("tmpe", 2)
        ob = [ph.sb("ob%d" % i, [128, D], F32) for i in range(2)]
        B_ob = ph.bufs_n("ob", 2)
        xtm = ph.sb("xtm", [128, 4, D], F32)
        B_x = ph.bufs_n("x", 4)
        hb = [ph.sb("hb%d" % i, [128, D], BF16) for i in range(2)]
        B_hb = ph.bufs_n("hb", 2)
        junk = ph.sb("junk", [128, D], BF16)
        B_junk = ph.buf("junk")
        ph.begin()
        for cc in range(4):
            P.dma("sp", xtm[:, cc, :], x_d[r0 + cc * 128:r0 + (cc + 1) * 128, :], writes=[B_x[cc]])
        for cc in range(4):
            P.dma("sp", pstg[:, cc, :], p_d[r0 + cc * 128:r0 + (cc + 1) * 128, :], writes=[B_pstg[cc]])
        phL = Phase()
        sq2 = [phL.sb("sq2%d" % i, [128, TT], BF16) for i in range(2)]
        B_sq2 = phL.bufs_n("sq2", 2)
        mean = phL.sb("mean", [128, TT], F32)
        msq = phL.sb("msq", [128, TT], F32)
        rstd = phL.sb("rstd", [128, TT], F32)
        B_mean, B_msq, B_rstd = phL.buf("mean"), phL.buf("msq"), phL.buf("rstd")
        xm = [phL.sb("xm%d" % i, [128, TT], F32) for i in range(2)]
        B_xm = phL.bufs_n("xm", 2)
        sl_ = [phL.sb("sl%d" % i, [128, TT], BF16) for i in range(2)]
        B_sl = phL.bufs_n("sl", 2)
        phL.begin()
        DBG('dbg_cv', cvb[:], B_cv, t)
        b1 = bank()
        b2 = bank()
        for c in range(8):
            a = c % 2
            MM(ps[b1][:, :], ones_bf[:], cvb[:, c, :], c == 0, c == 7, [B_cv[c], B_c], [B_ps[b1]])
            ACT(sq2[a][:], cvb[:, c, :], AF.Square, [B_cv[c]], [B_sq2[a]])
            MM(ps[b2][:, :], ones_bf[:], sq2[a][:], c == 0, c == 7, [B_sq2[a], B_c], [B_ps[b2]])
        ACT(mean[:], ps[b1][:, :], AF.Copy, [B_ps[b1]], [B_mean], scale=1.0 / 1024)
        ACT(msq[:], mean[:], AF.Square, [B_mean], [B_msq])
        STT(rstd[:], ps[b2][:, :], 1.0 / 1024, msq[:], ALU.mult, ALU.subtract, [B_ps[b2], B_msq], [B_rstd])
        ACT(rstd[:], rstd[:], AF.Ln, [B_rstd], [B_rstd], bias=EPS)
        ACT(rstd[:], rstd[:], AF.Exp, [B_rstd], [B_rstd], scale=-0.5)
        def g_pr():
            for g in (14, 15):
                def ev(j, b, g=g):
                    c = (g - 14) * 4 + j
                    ACT(sgd[:, c, :], ps[b][:, :], AF.Sigmoid, [B_ps[b]], [B_sgd[c]])
                yield from proj_units(("in", g), hT, B_hT, ev)
            for g in (16, 17):
                def ev(j, b, g=g):
                    c = (g - 16) * 4 + j
                    ACT(sgc[:, c, :], ps[b][:, :], AF.Sigmoid, [B_ps[b]], [B_sgc[c]])
                yield from proj_units(("in", g), hT, B_hT, ev)
            for g in range(2):
                def ev(j, b, g=g):
                    c = g * 4 + j
                    TTo("dve", mT[:, c, :], ps[b][:, :], sgd[:, c, :], ALU.mult, [B_ps[b], B_sgd[c]], [B_m[c]])
                yield from proj_units(("dn", g), ogT, B_og, ev)

        def g_ln():
            for c in range(8):
                a = c % 2
                TTo("dve", xm[a][:], cvb[:, c, :], mean[:], ALU.subtract, [B_cv[c], B_mean], [B_xm[a]])
                TTo("dve", xm[a][:], xm[a][:], rstd[:], ALU.mult, [B_xm[a], B_rstd], [B_xm[a]])
                ACT(sl_[a][:], xm[a][:], AF.Silu, [B_xm[a], B_c], [B_sl[a]], scale=lng[:, c:c + 1], bias=lnb[:, c:c + 1])
                TTo("dve", cgT[:, c, :], sl_[a][:], szc[:, c, :], ALU.mult, [B_sl[a], B_szc[c]], [B_cg[c]])
                yield

        gp, gl = g_pr(), g_ln()
        nunit = 0
        for _ in gp:
            nunit += 1
            if nunit >= 6 and nunit % 2 == 0:
                next(gl, None)
        for _ in gl:
            pass
        DBG('dbg_cg', cgT[:], B_cg, t)
        phL.end()
        for g in range(2):
            def ev(j, b, g=g):
                c = g * 4 + j
                a = c % 2
                TTo("dve", tmpm[a][:], ps[b][:, :], sgc[:, c, :], ALU.mult, [B_ps[b], B_sgc[c]], [B_tmpm[a]])
                TTo("dve", mT[:, c, :], mT[:, c, :], tmpm[a][:], ALU.add, [B_m[c], B_tmpm[a]], [B_m[c]])
            proj_fm(("cf", g), cgT, B_cg, ev)
        Hn = head_alloc(t + 1) if t + 1 < n_tiles else None
        for cc in range(4):
            CP("pool", pbf[:], pstg[:, cc, :], [B_pstg[cc]], [B_pbf])
            b = bank()
            for k in range(2):
                P.op("pe", lambda e, o_=psb[b][:, k * 128:(k + 1) * 128], i_=pbf[:, k * 128:(k + 1) * 128]:
                     e.transpose(out=o_, in_=i_, identity=ident[:]), [B_pbf, B_c], [B_ps[b]])
            CP("act", pT[:, :, cc * 128:(cc + 1) * 128], psb[b][:, 0:256].rearrange("p (k t) -> p k t", k=2),
               [B_ps[b]], [B_pT[cc]])
        wts = [w_get(("out", 0)), w_get(("out", 1), hold=1)]
        def ple_T(cc):
            i = cc % 2
            b = bank()
            for k in range(8):
                P.op("pe", lambda e, o_=psb[b][:, k * 128:(k + 1) * 128], i_=hb[i][:, k * 128:(k + 1) * 128]:
                     e.transpose(out=o_, in_=i_, identity=ident[:]), [B_hb[i], B_c], [B_ps[b]])
            TTo("dve", hT[:, :, cc * 128:(cc + 1) * 128], psb[b][:, 0:1024].rearrange("p (k t) -> p k t", k=8),
                bass.AP(png, 0, [[8, 128], [1, 8], [0, 128]]), ALU.mult, [B_ps[b], B_c], [B_hT[cc]])

        for cc in range(4):
            for nh in range(2):
                wt, wb = wts[nh]
                b = bank()
                for k in range(8):
                    MM(ps[b][:, :], mT[:, k, cc * 128:(cc + 1) * 128], wt[:, k, :], k == 0, k == 7, [wb] + B_m, [B_ps[b]])
                xs = xtm[:, cc, nh * 512:(nh + 1) * 512]
                TTo("dve", xs, xs, ps[b][:, :], ALU.add, [B_ps[b], B_x[cc]], [B_x[cc]])
            i = cc % 2
            ACT(junk[:], xtm[:, cc, :], AF.Square, [B_x[cc]], [B_junk, B_stp[cc]], accum_out=st4p[:, cc:cc + 1])
            ACT(st4p[:, 4 + cc:5 + cc], st4p[:, cc:cc + 1], AF.Ln, [B_stp[cc]], [B_stp[cc]], scale=1.0 / D, bias=EPS)
            ACT(st4p[:, 4 + cc:5 + cc], st4p[:, 4 + cc:5 + cc], AF.Exp, [B_stp[cc]], [B_stp[cc]], scale=-0.5)
            if cc >= 1:
                ple_T(cc - 1)
            ACT(hb[i][:], xtm[:, cc, :], AF.Copy, [B_x[cc], B_stp[cc]], [B_hb[i]], scale=st4p[:, 4 + cc:5 + cc])
        ple_T(3)
        DBG('dbg_m', mT[:], B_m, t)
        DBG('dbg_x1', xtm[:], B_x, t)
        if Hn is not None:
            head_act(Hn)
        for nh in range(2):
            if nh == 1 and Hn is not None:
                head_pe(Hn)
            wg, wgb = w_get(("pg", nh))
            wp, wpb = w_get(("pp", nh), hold=1)
            for cc in range(4):
                a = cc % 2
                b = bank()
                for k in range(8):
                    MM(ps[b][:, :], hT[:, k, cc * 128:(cc + 1) * 128], wg[:, k, :], k == 0, k == 7, [wgb, B_hT[cc]], [B_ps[b]])
                ACT(sgp[a][:], ps[b][:, :], AF.Sigmoid, [B_ps[b]], [B_sgp[a]])
                b = bank()
                for k in range(2):
                    MM(ps[b][:, :], pT[:, k, cc * 128:(cc + 1) * 128], wp[:, k, :], k == 0, k == 1, [wpb, B_pT[cc]], [B_ps[b]])
                TTo("dve", tmpe[a][:], ps[b][:, :], sgp[a][:], ALU.mult, [B_ps[b], B_sgp[a]], [B_tmpe[a]])
                xs = xtm[:, cc, nh * 512:(nh + 1) * 512]
                TTo("dve", xs, xs, tmpe[a][:], ALU.add, [B_tmpe[a], B_x[cc]], [B_x[cc]])
        for cc in range(4):
            ACT(junk[:], xtm[:, cc, :], AF.Square, [B_x[cc]], [B_junk, B_st4], accum_out=st4[:, cc:cc + 1])
        ACT(st4[:, 4:8], st4[:, 0:4], AF.Ln, [B_st4], [B_st4], scale=1.0 / D, bias=EPS)
        ACT(st4[:, 4:8], st4[:, 4:8], AF.Exp, [B_st4], [B_st4], scale=-0.5)
        for cc in range(4):
            a = cc % 2
            STT(ob[a][:], xtm[:, cc, :], st4[:, 4 + cc:5 + cc], fg[:], ALU.mult, ALU.mult, [B_x[cc], B_st4, B_c], [B_ob[a]])
            out_dmas.append(P.dma("pool", y_d[r0 + cc * 128:r0 + (cc + 1) * 128, :], ob[a][:], reads=[B_ob[a]]))
        ph.end()
        phC.end()

    P.emit(final_wait_ops=out_dmas)
    pst.close()
    return nc


def _consts():
    bf = ml_dtypes.bfloat16
    j = np.arange(128)
    c = {}
    c["c_ident"] = np.eye(128, dtype=np.float32).astype(bf)
    c["c_ones"] = np.ones((128, 128), np.float32).astype(bf)
    c["c_onesf"] = np.ones((128, 128), np.float32)
    c["c_ltri"] = (j[:, None] <= j[None, :]).astype(np.float32)
    c["c_ltrib"] = c["c_ltri"].astype(bf)
    ugt = (j[:, None] > j[None, :]).astype(np.float32)
    c["c_ugt"] = np.ascontiguousarray(np.broadcast_to(ugt[:, None, :], (128, 8, 128)))
    c["c_negm"] = np.tile(np.where(j[None, :] < j[:, None], -30000.0, 0.0).astype(np.float32), (1, 4)).astype(bf)
    strict = (j[None, :] > j[:, None]).astype(np.float32)
    c["c_strict"] = np.tile(strict, (1, 8)).astype(bf)
    c["c_identt"] = np.tile(np.eye(128, dtype=np.float32), (1, 8)).astype(bf)
    esel = np.zeros((128, 16, 64), np.float32)
    bsel = np.zeros((64, 16, 128), np.float32)
    for i in range(16):
        row = i if i < 8 else 32 + (i - 8)
        esel[:, i, row] = 1.0
        bsel[row, i, :] = 1.0
    c["c_esel"] = esel.astype(bf)
    c["c_bsel"] = bsel.astype(bf)
    return c


def _pk(v):
    return np.ascontiguousarray(np.asarray(v, np.float32).reshape(8, 128).T)


_NC_CACHE = {}


def kernel(x, p, mix_norm_g, w_in, dn_conv_w, dn_a_log, dn_dt_bias, dn_out_norm_g, w_dn_out,
           cf_dw_w, cf_dw_b, cf_ln_g, cf_ln_b, w_cf_out, w_out, ple_norm_g, w_ple_gate,
           w_ple_proj, final_norm_g):
    x = np.asarray(x, np.float32)
    p = np.asarray(p, np.float32)
    n_cores = 8
    if "nc" not in _NC_CACHE:
        _NC_CACHE["nc"] = build_program()
    nc = _NC_CACHE["nc"]
    f = lambda a: np.ascontiguousarray(np.asarray(a, np.float32))
    shared = dict(
        w_in=f(w_in[0]), w_dn_out=f(w_dn_out[0]), w_cf_out=f(w_cf_out[0]), w_out=f(w_out[0]),
        w_ple_gate=f(w_ple_gate[0]), w_ple_proj=f(w_ple_proj[0]),
        mix_norm_g=_pk(mix_norm_g[0]), ple_norm_g=_pk(ple_norm_g[0]),
        dn_out_norm_g=f(np.asarray(dn_out_norm_g[0]).reshape(128, 1)),
        dn_conv_w=f(np.asarray(dn_conv_w[0]).reshape(4, 24, 128).transpose(2, 1, 0)),
        cf_dw_w=f(np.asarray(cf_dw_w[0]).reshape(31, 8, 128).transpose(2, 1, 0)),
        cf_dw_b=_pk(cf_dw_b[0]), cf_ln_g=_pk(cf_ln_g[0]), cf_ln_b=_pk(cf_ln_b[0]),
        dn_a_log=f(np.broadcast_to(np.asarray(dn_a_log[0])[None, :], (128, 8))),
        dn_dt_bias=f(np.broadcast_to(np.asarray(dn_dt_bias[0])[None, :], (128, 8))),
        final_norm_g=f(np.broadcast_to(np.asarray(final_norm_g)[None, :], (128, D))),
    )
    shared.update(_consts())
    in_maps = []
    for b in range(n_cores):
        m = dict(shared)
        m["x"] = f(x[b])
        m["p"] = f(p[0, b])
        in_maps.append(m)
    res = run_bass_kernel_spmd(nc, in_maps, core_ids=list(range(n_cores)))
    out = np.stack([np.asarray(r["y"], np.float32) for r in res.results], axis=0)
    return out
```

```python
import contextlib
import math
import numpy as np
import ml_dtypes
import concourse.bass as bass
import concourse.mybir as mybir
from concourse.bass_utils import run_bass_kernel_spmd

F32 = mybir.dt.float32
BF16 = mybir.dt.bfloat16
AF = mybir.ActivationFunctionType
ALU = mybir.AluOpType

NSLOT = 32
T_CORE = 4096
D = 1024
TT = 512
NIN = 9232
EPS = 1e-6


class Buf:
    __slots__ = ("name", "w", "r")

    def __init__(self, name):
        self.name = name
        self.w = {}
        self.r = {}


class Op:
    __slots__ = ("eng", "fn", "reads", "writes", "dma", "eidx", "waits", "signal", "semkey", "sigval", "gidx", "event")


def _later(a, b):
    return a if a.gidx > b.gidx else b


class Prog:
    ENGS = ("pe", "act", "dve", "pool", "sp")

    def __init__(self, nc):
        self.nc = nc
        self.ops = []
        self.per_eng = {e: [] for e in self.ENGS}
        self.ndma = 0
        self.arena = Buf("ARENA")

    def op(self, eng, fn, reads=(), writes=(), dma=False):
        o = Op()
        o.eng = eng
        o.fn = fn
        o.reads = [b for b in reads if b is not None]
        o.writes = [b for b in writes if b is not None]
        o.dma = dma
        o.waits = []
        o.signal = dma
        o.semkey = None
        o.sigval = None
        o.event = None
        o.gidx = len(self.ops)
        o.eidx = len(self.per_eng[eng])
        if dma:
            k = self.ndma
            self.ndma += 1
            o.semkey = ("slot", k % NSLOT)
            o.sigval = 16 * (k // NSLOT + 1)
        self.ops.append(o)
        self.per_eng[eng].append(o)
        return o

    def event(self, kind, bufs):
        o = Op()
        o.eng = None
        o.event = (kind, list(bufs))
        o.gidx = len(self.ops)
        self.ops.append(o)

    def dma(self, eng, out, in_, reads=(), writes=()):
        return self.op(eng, lambda e: e.dma_start(out=out, in_=in_), reads, writes, dma=True)

    def _key(self, o):
        return o.semkey if o.dma else o.eng

    def analyze(self):
        A = self.arena
        for o in self.ops:
            if o.event is not None:
                kind, bufs = o.event
                if kind == "alias":
                    new_b, old_b = bufs
                    for b in new_b:
                        for ob in old_b:
                            for k, d in ob.w.items():
                                b.w[k] = _later(b.w[k], d) if k in b.w else d
                            for k, d in ob.r.items():
                                b.r[k] = _later(b.r[k], d) if k in b.r else d
                elif kind == "free":
                    for b in bufs:
                        for k, d in b.w.items():
                            A.w[k] = _later(A.w[k], d) if k in A.w else d
                        for k, d in b.r.items():
                            A.r[k] = _later(A.r[k], d) if k in A.r else d
                else:
                    for b in bufs:
                        b.w = dict(A.w)
                        b.r = dict(A.r)
                continue
            raw = set()
            other = set()
            for b in o.reads:
                raw.update(b.w.values())
            for b in o.writes:
                other.update(b.w.values())
                other.update(b.r.values())
            other -= raw
            need = []
            for d, is_raw in [(d, True) for d in raw] + [(d, False) for d in other]:
                if d is o:
                    continue
                if d.dma:
                    need.append(d)
                elif d.eng == o.eng and not o.dma:
                    if is_raw and o.eng != "pe" and (o.eidx - d.eidx) <= 3:
                        need.append(d)
                else:
                    need.append(d)
            for d in need:
                d.signal = True
            o.waits = need
            k = self._key(o)
            for b in o.reads:
                b.r[k] = o
            for b in o.writes:
                b.w = {k: o}
                b.r = {}
        cnt = {e: 0 for e in self.ENGS}
        for o in self.ops:
            if o.event is None and not o.dma and o.signal:
                cnt[o.eng] += 1
                o.semkey = ("eng", o.eng)
                o.sigval = cnt[o.eng]

    def emit(self, final_wait_ops=()):
        nc = self.nc
        self.analyze()
        with contextlib.ExitStack() as st:
            sems = {}
            for e in self.ENGS:
                sems[("eng", e)] = st.enter_context(nc.semaphore("s_" + e))
            for j in range(NSLOT):
                sems[("slot", j)] = st.enter_context(nc.semaphore("d_%d" % j))
            block = st.enter_context(nc.Block())

            def run(engname, eng):
                seen = {}
                for o in self.per_eng[engname]:
                    wl = {}
                    for d in o.waits:
                        wl[d.semkey] = max(wl.get(d.semkey, 0), d.sigval)
                    if o.dma and o.sigval > 16:
                        wl[o.semkey] = max(wl.get(o.semkey, 0), o.sigval - 16)
                    for key, val in wl.items():
                        if seen.get(key, 0) >= val:
                            continue
                        eng.wait_ge(sems[key], val)
                        seen[key] = val
                    ins = o.fn(eng)
                    if o.signal:
                        ins.then_inc(sems[o.semkey], 16 if o.dma else 1)
                if engname == "sp":
                    for o in final_wait_ops:
                        if seen.get(o.semkey, 0) < o.sigval:
                            eng.wait_ge(sems[o.semkey], o.sigval)
                            seen[o.semkey] = o.sigval

            @block.sync
            def _(e):
                run("sp", e)

            @block.tensor
            def _(e):
                run("pe", e)

            @block.scalar
            def _(e):
                run("act", e)

            @block.vector
            def _(e):
                run("dve", e)

            @block.gpsimd
            def _(e):
                run("pool", e)


def build_program(n_tiles=T_CORE // TT, debug=False):
    nc = bass.Bass("TRN2", target_bir_lowering=False)
    P = Prog(nc)
    Tn = n_tiles * TT

    def din(name, shape, dt=F32):
        return nc.dram_tensor(name, shape, dt, kind="ExternalInput").ap()

    x_d = din("x", [Tn, D])
    p_d = din("p", [Tn, 256])
    w_in_d = din("w_in", [D, NIN])
    w_dn_d = din("w_dn_out", [D, D])
    w_cf_d = din("w_cf_out", [D, D])
    w_out_d = din("w_out", [D, D])
    w_pg_d = din("w_ple_gate", [D, D])
    w_pp_d = din("w_ple_proj", [256, D])
    mng_d = din("mix_norm_g", [128, 8])
    png_d = din("ple_norm_g", [128, 8])
    dng_d = din("dn_out_norm_g", [128, 1])
    cw_d = din("dn_conv_w", [128, 24, 4])
    cfw_d = din("cf_dw_w", [128, 8, 31])
    cfb_d = din("cf_dw_b", [128, 8])
    lng_d = din("cf_ln_g", [128, 8])
    lnb_d = din("cf_ln_b", [128, 8])
    alog_d = din("dn_a_log", [128, 8])
    dtb_d = din("dn_dt_bias", [128, 8])
    fg_d = din("final_norm_g", [128, D])
    c_ident_d = din("c_ident", [128, 128], BF16)
    c_ones_d = din("c_ones", [128, 128], BF16)
    c_onesf_d = din("c_onesf", [128, 128])
    c_ltri_d = din("c_ltri", [128, 128])
    c_ltrib_d = din("c_ltrib", [128, 128], BF16)
    c_ugt_d = din("c_ugt", [128, 8, 128])
    c_negm_d = din("c_negm", [128, 512], BF16)
    c_strict_d = din("c_strict", [128, 1024], BF16)
    c_identt_d = din("c_identt", [128, 1024], BF16)
    c_esel_d = din("c_esel", [128, 16, 64], BF16)
    c_bsel_d = din("c_bsel", [64, 16, 128], BF16)
    y_d = nc.dram_tensor("y", [Tn, D], F32, kind="ExternalOutput").ap()

    dbg = {}
    if debug:
        for nm_, shp_, dt_ in [("dbg_hT", [128, 8, TT], BF16), ("dbg_qkv", [128, 24, 4 + TT], BF16),
                               ("dbg_sz", [128, 8, TT], BF16), ("dbg_og", [128, 8, TT], BF16),
                               ("dbg_cg", [128, 8, TT], BF16), ("dbg_m", [128, 8, TT], BF16),
                               ("dbg_x1", [128, 4, D], F32), ("dbg_cv", [128, 8, TT], BF16),
                               ("dbg_sm", [128, 16, 8], F32), ("dbg_T2T", [128, 1024], BF16),
                               ("dbg_DT", [128, 1024], BF16), ("dbg_of", [128, 8, 128], F32),
                               ("dbg_X0", [128, 1024], BF16), ("dbg_vn", [128, 8, 128], BF16)]:
            dbg[nm_] = nc.dram_tensor(nm_, shp_, dt_, kind="ExternalOutput").ap()

    def DBG(name, ap, reads, t=0, cond=True):
        if debug and cond and t == 0:
            out_dmas.append(P.dma("sp", dbg[name], ap, reads=reads))

    out_dmas = []

    def dscr(name, shape, dt=BF16):
        return nc.dram_tensor(name, shape, dt, kind="Internal").ap()

    winb = dscr("winb", [18, 128, 8, 512])
    wbdb = dscr("wbdb", [128, 8, 16])
    wdnb = dscr("wdnb", [2, 128, 8, 512])
    wcfb = dscr("wcfb", [2, 128, 8, 512])
    woutb = dscr("woutb", [2, 128, 8, 512])
    wpgb = dscr("wpgb", [2, 128, 8, 512])
    wppb = dscr("wppb", [2, 128, 2, 512])
    dgs = dscr("dgs", [8, 128, 31, 128])
    B_winb = [Buf("winb%d" % i) for i in range(18)]
    B_wbdb = Buf("wbdb")
    B_w2 = {n: [Buf(n + "0"), Buf(n + "1")] for n in ("dn", "cf", "out", "pg", "pp")}
    B_dgs = [Buf("dgs%d" % i) for i in range(8)]

    pst = contextlib.ExitStack()

    def sbp(name, shape, dt):
        return pst.enter_context(nc.sbuf_tensor(name, shape, dt))

    uid = [0]

    class Phase:
        def __init__(self):
            self.st = contextlib.ExitStack()
            self.bufs = []

        def sb(self, name, shape, dt):
            uid[0] += 1
            return self.st.enter_context(nc.sbuf_tensor("%s_%d" % (name, uid[0]), shape, dt))

        def buf(self, name):
            b = Buf(name)
            self.bufs.append(b)
            return b

        def bufs_n(self, name, n):
            return [self.buf("%s%d" % (name, i)) for i in range(n)]

        def begin(self):
            P.event("alloc", self.bufs)

        def end(self):
            P.event("free", self.bufs)
            self.st.close()

    def ACT(out, in_, func, reads, writes, **kw):
        return P.op("act", lambda e: e.activation(out=out, in_=in_, func=func, **kw), reads, writes)

    def TTo(eng, out, in0, in1, op, reads, writes):
        return P.op(eng, lambda e: e.tensor_tensor(out=out, in0=in0, in1=in1, op=op), reads, writes)

    def TS(eng, out, in0, s1, s2, op0, op1, reads, writes):
        if op1 is None:
            return P.op(eng, lambda e: e.tensor_scalar(out=out, in0=in0, scalar1=s1, scalar2=None, op0=op0), reads, writes)
        return P.op(eng, lambda e: e.tensor_scalar(out=out, in0=in0, scalar1=s1, scalar2=s2, op0=op0, op1=op1), reads, writes)

    def STT(out, in0, scalar, in1, op0, op1, reads, writes):
        return P.op("dve", lambda e: e.scalar_tensor_tensor(out=out, in0=in0, scalar=scalar, in1=in1, op0=op0, op1=op1), reads, writes)

    def MM(out, lhsT, rhs, start, stop, reads, writes):
        return P.op("pe", lambda e: e.matmul(out, lhsT=lhsT, rhs=rhs, start=start, stop=stop), reads, writes)

    def CP(eng, out, in_, reads, writes):
        if eng == "act":
            return P.op("act", lambda e: e.activation(out=out, in_=in_, func=AF.Copy), reads, writes)
        return P.op(eng, lambda e: e.tensor_copy(out=out, in_=in_), reads, writes)

    def MEMSET(eng, ap, val, writes):
        return P.op(eng, lambda e: e.memset(ap, val), (), writes)

    def bc8(t, n):
        return bass.AP(t, 0, [[8, 128], [1, 8], [0, n]])

    ident = sbp("ident", [128, 128], BF16)
    ones_bf = sbp("ones_bf", [128, 128], BF16)
    ones_f = sbp("ones_f", [128, 128], F32)
    ltri = sbp("ltri", [128, 128], F32)
    ltrib = sbp("ltrib", [128, 128], BF16)
    ugt = sbp("ugt", [128, 8, 128], F32)
    negm = sbp("negm", [128, 512], BF16)
    strictm = sbp("strictm", [128, 1024], BF16)
    identt = sbp("identt", [128, 1024], BF16)
    esel = sbp("esel", [128, 16, 64], BF16)
    bsel = sbp("bsel", [64, 16, 128], BF16)
    fg = sbp("fg", [128, D], F32)
    mng = sbp("mng", [128, 8], F32)
    png = sbp("png", [128, 8], F32)
    dng = sbp("dng", [128, 1], F32)
    cw = sbp("cw", [128, 24, 4], F32)
    cfw = sbp("cfw", [128, 8, 31], F32)
    cfb = sbp("cfb", [128, 8], F32)
    lng = sbp("lng", [128, 8], F32)
    lnb = sbp("lnb", [128, 8], F32)
    alog = sbp("alog", [128, 8], F32)
    dtb = sbp("dtb", [128, 8], F32)
    negA = sbp("negA", [128, 8], F32)
    wbd = sbp("wbd", [128, 8, 16], BF16)
    B_c = Buf("consts")
    B_negA = Buf("negA")
    B_wbd = Buf("wbd")
    B_cl = []
    for t_, d_ in [(ident, c_ident_d), (ones_bf, c_ones_d), (ones_f, c_onesf_d), (ltri, c_ltri_d), (ltrib, c_ltrib_d), (ugt, c_ugt_d),
                   (negm, c_negm_d), (strictm, c_strict_d), (identt, c_identt_d), (esel, c_esel_d), (bsel, c_bsel_d),
                   (fg, fg_d), (mng, mng_d), (png, png_d), (dng, dng_d), (cw, cw_d), (cfw, cfw_d), (cfb, cfb_d),
                   (lng, lng_d), (lnb, lnb_d), (alog, alog_d), (dtb, dtb_d)]:
        bb_ = Buf("c%d" % len(B_cl))
        B_cl.append(bb_)
        P.dma("sp", t_[:], d_, writes=[bb_])
    cdummy = sbp("cdummy", [128, 8], F32)
    P.op("pool", lambda e: e.memset(cdummy[:], 0.0), B_cl, [B_c])
    ACT(negA[:], alog[:], AF.Exp, [B_c], [B_negA])
    TS("dve", negA[:], negA[:], -1.0, None, ALU.mult, None, [B_negA], [B_negA])

    hTs = [sbp("hT%d" % i, [128, 8, TT], BF16) for i in range(2)]
    B_hTs = [[Buf("hT%d_%d" % (j, i)) for i in range(4)] for j in range(2)]
    st4h = sbp("st4h", [128, 8], F32)
    B_st4h = Buf("st4h")
    st4p = sbp("st4p", [128, 8], F32)
    B_stp = [Buf("st4p%d" % i) for i in range(4)]
    NRING = 3
    wring = [sbp("wring%d" % i, [128, 8, 512], BF16) for i in range(NRING)]
    B_wr = [Buf("wr%d" % i) for i in range(NRING)]
    S_f = sbp("S_f", [128, 8, 128], F32)
    S_b = sbp("S_b", [128, 8, 128], BF16)
    B_Sf = Buf("Sf")
    B_Sb = Buf("Sb")
    hist = sbp("hist", [128, 24, 4], BF16)
    B_hist = [Buf("hist%d" % i) for i in range(24)]
    cbuf = sbp("cbuf", [128, 8, 32 + TT], BF16)
    B_cb = [Buf("cb%d" % i) for i in range(8)]
    ogT = sbp("ogT", [128, 8, TT], BF16)
    B_og = [Buf("og%d" % i) for i in range(4)]
    st4 = sbp("st4", [128, 8], F32)
    B_st4 = Buf("st4")

    ps = [pst.enter_context(nc.psum_tensor("ps%d" % i, [128, 512], F32)) for i in range(8)]
    psb = [t_.bitcast(BF16) for t_ in ps]
    B_ps = [Buf("ps%d" % i) for i in range(8)]
    bank_ctr = [0]

    def bank():
        i = bank_ctr[0] % 7
        bank_ctr[0] += 1
        return i

    MEMSET("pool", S_f[:], 0.0, [B_Sf])
    MEMSET("pool", S_b[:], 0.0, [B_Sb])
    MEMSET("pool", hist[:], 0.0, B_hist)
    MEMSET("pool", cbuf[:], 0.0, B_cb)

    def head_alloc(tt_):
        phH_ = Phase()
        H = {"ph": phH_, "t": tt_}
        H["xtm"] = phH_.sb("xtmh", [128, 4, D], F32)
        H["B_x"] = phH_.bufs_n("xh", 4)
        H["hb"] = [phH_.sb("hbh%d" % i, [128, D], BF16) for i in range(4)]
        H["B_hb"] = phH_.bufs_n("hbh", 4)
        H["junk"] = H["hb"][3]
        H["B_junk"] = H["B_hb"][3]
        phH_.begin()
        rr = tt_ * TT
        for cc in range(4):
            P.dma("sp", H["xtm"][:, cc, :], x_d[rr + cc * 128:rr + (cc + 1) * 128, :], writes=[H["B_x"][cc]])
        return H

    def head_act(H):
        xtm_, B_x_, hb_, B_hb_, junk_, B_junk_ = H["xtm"], H["B_x"], H["hb"], H["B_hb"], H["junk"], H["B_junk"]
        for cc in range(4):
            ACT(junk_[:], xtm_[:, cc, :], AF.Square, [B_x_[cc]], [B_junk_, B_st4h], accum_out=st4h[:, cc:cc + 1])
        ACT(st4h[:, 4:8], st4h[:, 0:4], AF.Ln, [B_st4h], [B_st4h], scale=1.0 / D, bias=EPS)
        ACT(st4h[:, 4:8], st4h[:, 4:8], AF.Exp, [B_st4h], [B_st4h], scale=-0.5)
        for cc in range(4):
            ACT(hb_[cc][:], xtm_[:, cc, :], AF.Copy, [B_x_[cc], B_st4h], [B_hb_[cc]], scale=st4h[:, 4 + cc:5 + cc])

    def head_pe(H):
        hb_, B_hb_ = H["hb"], H["B_hb"]
        hT_, B_hT_ = hTs[H["t"] % 2], B_hTs[H["t"] % 2]
        for cc in range(4):
            b = bank()
            for k in range(8):
                P.op("pe", lambda e, o_=psb[b][:, k * 128:(k + 1) * 128], i_=hb_[cc][:, k * 128:(k + 1) * 128]:
                     e.transpose(out=o_, in_=i_, identity=ident[:]), [B_hb_[cc], B_c], [B_ps[b]])
            TTo("dve", hT_[:, :, cc * 128:(cc + 1) * 128], psb[b][:, 0:1024].rearrange("p (k t) -> p k t", k=8),
                bass.AP(mng, 0, [[8, 128], [1, 8], [0, 128]]), ALU.mult, [B_ps[b], B_c], [B_hT_[cc]])
        H["ph"].end()

    def head_compute(H):
        head_act(H)
        head_pe(H)


    head_compute(head_alloc(0))

    dcs = dscr("dcs", [3, 128, 8, 512])
    B_dcs = [Buf("dcs%d" % i) for i in range(3)]
    ph = Phase()
    prep_lo = nc.sbuf_base
    dgt = [ph.sb("dgt%d" % i, [128, 32, 128], BF16) for i in range(2)]
    B_dgt = ph.bufs_n("dgt", 2)
    prep_hi = nc.sbuf_base
    ph.begin()
    w_in_v = w_in_d.rearrange("(k p) n -> p k n", p=128)
    P.dma("pool", wbd[:], w_in_v[:, :, 4096:4112], writes=[B_wbd])
    for g in range(3):
        a_ = g % 2
        TTo("dve", dgt[a_][:], bass.AP(ident, 0, [[128, 128], [0, 32], [1, 128]]),
            bass.AP(cw, g * 32, [[96, 128], [1, 32], [0, 128]]), ALU.mult, [B_c], [B_dgt[a_]])
        P.dma("sp", dcs[g].rearrange("p k (a m) -> p (k a) m", m=128), dgt[a_][:], reads=[B_dgt[a_]], writes=[B_dcs[g]])
    for c in range(8):
        a_ = (c + 1) % 2
        TTo("dve", dgt[a_][:, 0:31, :], bass.AP(ident, 0, [[128, 128], [0, 31], [1, 128]]),
            bass.AP(cfw, c * 31, [[248, 128], [1, 31], [0, 128]]), ALU.mult, [B_c], [B_dgt[a_]])
        P.dma("sp", dgs[c], dgt[a_][:, 0:31, :], reads=[B_dgt[a_]], writes=[B_dgs[c]])
    ph.st.close()

    order = []
    for g in range(8):
        order.append(("in", g))
    order += [("dc", 0), ("dc", 1), ("dc", 2)]
    for g in (10, 11, 8, 9, 12, 13):
        order.append(("in", g))
    order += [("dg", c) for c in range(8)]
    for g in (14, 15, 16, 17):
        order.append(("in", g))
    for nm in ("dn", "cf", "out"):
        order += [(nm, 0), (nm, 1)]
    order += [("pg", 0), ("pp", 0), ("pg", 1), ("pp", 1)]
    NG = len(order)
    wstate = {"issued": 0, "used": 0}

    w2v = {"dn": w_dn_d.rearrange("(k p) n -> p k n", p=128), "cf": w_cf_d.rearrange("(k p) n -> p k n", p=128),
           "out": w_out_d.rearrange("(k p) n -> p k n", p=128), "pg": w_pg_d.rearrange("(k p) n -> p k n", p=128),
           "pp": w_pp_d.rearrange("(k p) n -> p k n", p=128)}
    scr2 = {"dn": wdnb, "cf": wcfb, "out": woutb, "pg": wpgb, "pp": wppb}

    pending_store = {}

    def w_issue():
        i = wstate["issued"]
        if i >= NG * n_tiles:
            return
        nm, g = order[i % NG]
        slot = i % NRING
        first = i < NG
        dst = wring[slot][:, 0:2, :] if nm == "pp" else wring[slot][:]
        pend = pending_store.pop(slot, None)
        if pend is not None:
            P.dma("pool", pend[0], pend[1], reads=[B_wr[slot]], writes=[pend[2]])
        if nm == "dc":
            P.dma("sp", dst, dcs[g], reads=[B_dcs[g]], writes=[B_wr[slot]])
        elif nm == "dg":
            P.dma("sp", wring[slot][:].rearrange("p k n -> p (k n)")[:, 0:31 * 128], dgs[g].rearrange("p j m -> p (j m)"),
                  reads=[B_dgs[g]], writes=[B_wr[slot]])
        elif nm == "in":
            if first:
                s0 = g * 512 if g < 8 else 4112 + (g - 8) * 512
                P.dma("pool", dst, w_in_v[:, :, s0:s0 + 512], writes=[B_wr[slot]])
                if n_tiles > 1:
                    pending_store[slot] = (winb[g], dst, B_winb[g])
            else:
                P.dma("sp", dst, winb[g], reads=[B_winb[g]], writes=[B_wr[slot]])
        else:
            if first:
                P.dma("pool", dst, w2v[nm][:, :, g * 512:(g + 1) * 512], writes=[B_wr[slot]])
                if n_tiles > 1:
                    pending_store[slot] = (scr2[nm][g], dst, B_w2[nm][g])
            else:
                P.dma("sp", dst, scr2[nm][g], reads=[B_w2[nm][g]], writes=[B_wr[slot]])
        wstate["issued"] += 1

    def w_get(expect, hold=0):
        i = wstate["used"]
        assert order[i % NG] == expect, (order[i % NG], expect)
        while wstate["issued"] < min(i + NRING - hold, NG * n_tiles):
            w_issue()
        wstate["used"] += 1
        slot = i % NRING
        return wring[slot], B_wr[slot]

    def proj_fm(expect, rhs_t, rhs_bufs, evac):
        wt, wb = w_get(expect)
        for j in range(4):
            b = bank()
            for k in range(8):
                MM(ps[b][:, :], wt[:, k, j * 128:(j + 1) * 128], rhs_t[:, k, :], k == 0, k == 7,
                   [wb] + rhs_bufs, [B_ps[b]])
            evac(j, b)

    def proj_units(expect, rhs_t, rhs_bufs, evac):
        wt, wb = w_get(expect)
        for j in range(4):
            b = bank()
            for k in range(8):
                MM(ps[b][:, :], wt[:, k, j * 128:(j + 1) * 128], rhs_t[:, k, :], k == 0, k == 7,
                   [wb] + rhs_bufs, [B_ps[b]])
            evac(j, b)
            yield

    LNQ = math.log(128.0 ** -0.5)
    for t in range(n_tiles):
        r0 = t * TT
        hT, B_hT = hTs[t % 2], B_hTs[t % 2]
        DBG('dbg_hT', hT[:], B_hT, t)
        phc_lo = nc.sbuf_base
        phC = Phase()
        sgb = phC.sb("sgbcv", [128, 8, TT], BF16)
        cvb = sgb
        B_sgb = phC.bufs_n("sgbcv", 8)
        B_cv = B_sgb
        szc = phC.sb("szc", [128, 8, TT], BF16)
        B_szc = phC.bufs_n("szc", 8)
        if t == 0:
            assert prep_lo == phc_lo and nc.sbuf_base >= prep_hi, (prep_lo, phc_lo, prep_hi, nc.sbuf_base)
        phC.begin()
        if t == 0:
            P.event("alias", (phC.bufs, B_dgt))
        ph = Phase()
        qkvr = ph.sb("qkvr", [128, 24, 4 + TT], BF16)
        B_q = ph.bufs_n("q", 24)
        sz = ph.sb("sz", [128, 8, TT], BF16)
        B_sz = ph.bufs_n("sz", 8)
        def mkset(si):
            S = {}
            S["on"] = ph.sb("on%d" % si, [128, 1024], BF16)
            S["B_on"] = ph.buf("on")
            S["ldU"] = ph.sb("ldU%d" % si, [128, 8, 128], BF16)
            S["B_ldU"] = ph.buf("ldU")
            for nm_ in ("DT", "DTsb", "QKm"):
                S[nm_] = ph.sb("%s%d" % (nm_, si), [128, 1024], BF16)
                S["B_" + nm_] = ph.buf(nm_)
            S["DTs"], S["B_DTs"] = S["DTsb"], S["B_DTsb"]
            S["Yo"] = ph.sb("Yo%d" % si, [128, 8, 64], BF16)
            S["B_Yo"] = ph.buf("Yo")
            for nm_ in ("GK", "Kd", "Vt", "nww"):
                S[nm_] = ph.sb("%s%d" % (nm_, si), [128, 8, 128], BF16)
                S["B_" + nm_] = ph.buf(nm_)
            for nm_ in ("X", "Y", "W"):
                S[nm_ + "s"] = [ph.sb("%s%d_%d" % (nm_, si, i), [128, 1024], BF16) for i in range(2)]
                S["B_" + nm_] = [ph.bufs_n(nm_ + "a", 2), ph.bufs_n(nm_ + "b", 2)]
            return S

        SETS = [mkset(0), mkset(1)]
        vn = ph.sb("vn", [128, 8, 128], BF16)
        B_vn = ph.buf("vn")
        of = ph.sb("of", [128, 8, 128], F32)
        B_of = ph.buf("of")
        sqt = [SETS[i]["on"][:, 0:TT] for i in range(2)]
        B_sqt = [SETS[i]["B_on"] for i in range(2)]
        lnv = of[0:64, 0:4, :].rearrange("p h c -> p (h c)")
        B_lnv = B_of
        rsb = vn[0:64, 0:4, :].rearrange("p h c -> p (h c)")
        B_rsb = B_vn
        smc = [ph.sb("smc%d" % i, [128, 16, 8], F32) for i in range(4)]
        B_smc = ph.bufs_n("smc", 4)
        gsbc = [ph.sb("gsbc%d" % i, [128, 16], F32) for i in range(4)]
        ph.begin()
        BET, NEGB, TMPD, EE, SP, LD, GS, EGL, DGL, EGD, SSQ, RSO = range(12)

        for g in range(6):
            def ev(j, b, g=g):
                c = g * 4 + j
                CP("act" if (c % 2 == 0 or t == 0) else "dve", qkvr[:, c, 4:4 + TT], ps[b][:, :], [B_ps[b]], [B_q[c]])
            proj_fm(("in", g), hT, B_hT, ev)
        for g in (6, 7):
            def ev(j, b, g=g):
                c = (g - 6) * 4 + j
                ACT(sz[:, c, :], ps[b][:, :], AF.Silu, [B_ps[b]], [B_sz[c]])
            proj_fm(("in", g), hT, B_hT, ev)
        def bd_chain(cc):
            hs_ = slice(cc * 128, (cc + 1) * 128)
            sm, gsb, B_sm = smc[cc], gsbc[cc], B_smc[cc]

            def smv(i):
                return sm[:, i, :]

            bbd = bank()
            for k in range(8):
                MM(ps[bbd][:, 0:16], hT[:, k, hs_], wbd[:, k, :], k == 0, k == 7, B_hT + [B_wbd], [B_ps[bbd]])
            ACT(smv(BET), ps[bbd][:, 0:8], AF.Exp, [B_ps[bbd]], [B_sm], scale=-1.0)
            TS("dve", smv(BET), smv(BET), 1.0, None, ALU.add, None, [B_sm], [B_sm])
            P.op("dve", lambda e, o_=smv(BET): e.reciprocal(out=o_, in_=o_), [B_sm], [B_sm])
            TTo("dve", smv(TMPD), ps[bbd][:, 8:16], dtb[:], ALU.add, [B_ps[bbd], B_c], [B_sm])
            yield
            ACT(smv(EE), smv(TMPD), AF.Abs, [B_sm], [B_sm])
            ACT(smv(EE), smv(EE), AF.Exp, [B_sm], [B_sm], scale=-1.0)
            ACT(smv(SP), smv(EE), AF.Ln, [B_sm], [B_sm], bias=1.0)
            TS("dve", smv(DGL), smv(TMPD), 0.0, None, ALU.max, None, [B_sm], [B_sm])
            TTo("dve", smv(SP), smv(SP), smv(DGL), ALU.add, [B_sm], [B_sm])
            yield
            TTo("dve", smv(LD), smv(SP), negA[:], ALU.mult, [B_sm, B_negA], [B_sm])
            TS("dve", smv(NEGB), smv(BET), -1.0, None, ALU.mult, None, [B_sm], [B_sm])
            yield
            bg = bank()
            MM(ps[bg][:, 0:8], ltri[:], smv(LD), True, True, [B_sm, B_c], [B_ps[bg]])
            MM(ps[bg][:, 8:16], ones_f[:], smv(LD), True, True, [B_sm, B_c], [B_ps[bg]])
            CP("act", gsb[:], ps[bg][:, 0:16], [B_ps[bg]], [B_sm])
            yield
            ACT(smv(GS), gsb[:, 0:8], AF.Exp, [B_sm], [B_sm])
            ACT(smv(EGL), gsb[:, 8:16], AF.Exp, [B_sm], [B_sm])
            TTo("dve", smv(DGL), gsb[:, 8:16], gsb[:, 0:8], ALU.subtract, [B_sm], [B_sm])
            ACT(smv(EGD), smv(DGL), AF.Exp, [B_sm], [B_sm])
            yield

        _bdg = [bd_chain(c_) for c_ in range(4)]
        while _bdg:
            for g_ in list(_bdg):
                try:
                    next(g_)
                except StopIteration:
                    _bdg.remove(g_)
        for g in range(3):
            wt, wb = w_get(("dc", g))
            wflat = wt[:].rearrange("p k n -> p (k n)")
            for c8 in range(8):
                c = g * 8 + c8
                CP("pool", qkvr[:, c, 1:4], hist[:, c, 0:3], [B_hist[c]], [B_q[c]])
                b = bank()
                for j in range(4):
                    idx = c8 * 4 + j
                    MM(ps[b][:, :], wflat[:, idx * 128:(idx + 1) * 128], qkvr[:, c, 1 + j:1 + j + TT], j == 0, j == 3,
                       [wb, B_q[c]], [B_ps[b]])
                CP("pool", hist[:, c, 0:3], qkvr[:, c, 1 + TT:4 + TT], [B_q[c]], [B_hist[c]])
                ACT(qkvr[:, c, 4:4 + TT], ps[b][:, :], AF.Silu, [B_ps[b]], [B_q[c]])
        bss = bank()
        for i in range(16):
            a = i % 2
            ACT(sqt[a][:], qkvr[:, i, 4:4 + TT], AF.Square, [B_q[i]], [B_sqt[a]])
            MM(ps[bss][0:64, :], esel[:, i, :], sqt[a][:], i == 0, i == 15, [B_sqt[a], B_c], [B_ps[bss]])
        ACT(lnv[:, :], ps[bss][0:64, :], AF.Ln, [B_ps[bss]], [B_lnv], bias=EPS)
        MEMSET("pool", rsb[:], 0.0, [B_rsb])
        ACT(rsb[0:8, :], lnv[0:8, :], AF.Exp, [B_lnv], [B_rsb], scale=-0.5, bias=LNQ)
        ACT(rsb[32:40, :], lnv[32:40, :], AF.Exp, [B_lnv], [B_rsb], scale=-0.5)
        for i in range(16):
            b = bank()
            MM(ps[b][:, :], bsel[:, i, :], rsb[:, :], True, True, [B_rsb, B_c], [B_ps[b]])
            TTo("dve", qkvr[:, i, 4:4 + TT], qkvr[:, i, 4:4 + TT], ps[b][:, :], ALU.mult, [B_q[i], B_ps[b]], [B_q[i]])

        DBG('dbg_qkv', qkvr[:], B_q, t)
        DBG('dbg_sz', sz[:], B_sz, t)
        def chunk_A(cc, S):
            ts_ = slice(4 + cc * 128, 4 + (cc + 1) * 128)
            hs_ = slice(cc * 128, (cc + 1) * 128)
            sm, gsb, ldU, DT, DTs, DTsb, QKm = smc[cc], gsbc[cc], S["ldU"], S["DT"], S["DTs"], S["DTsb"], S["QKm"]
            on, B_on = S["on"], S["B_on"]
            GK, Kd, Vt, nww, Xs, Ys, Ws = S["GK"], S["Kd"], S["Vt"], S["nww"], S["Xs"], S["Ys"], S["Ws"]
            B_sm, B_ldU, B_DT, B_DTs, B_DTsb, B_QKm = B_smc[cc], S["B_ldU"], S["B_DT"], S["B_DTs"], S["B_DTsb"], S["B_QKm"]
            B_GK, B_Kd, B_Vt, B_nww, B_X, B_Y, B_W = S["B_GK"], S["B_Kd"], S["B_Vt"], S["B_nww"], S["B_X"], S["B_Y"], S["B_W"]
            T2T = Ws[1]
            B_T2T = B_W[1]
            Yo, B_Yo = S["Yo"], S["B_Yo"]

            def smv(i):
                return sm[:, i, :]

            TTo("dve", ldU[:], ugt[:], bass.AP(sm, LD * 8, [[128, 128], [1, 8], [0, 128]]), ALU.mult,
                [B_sm, B_c], [B_ldU])
            yield
            barg = [bank(), bank()]
            for i in range(2):
                MM(ps[barg[i]][:, :], ident[:], negm[:], True, False, [B_c], [B_ps[barg[i]]])
            for h in range(8):
                b = barg[h // 4]
                o_ = ps[b][:, (h % 4) * 128:(h % 4 + 1) * 128]
                MM(o_, ldU[:, h, :], ltrib[:], False, True, [B_ldU, B_c], [B_ps[b]])
            for i in range(2):
                ACT(DT[:, i * 512:(i + 1) * 512], ps[barg[i]][:, :], AF.Exp, [B_ps[barg[i]]], [B_DT])
            TTo("dve", DTs[:], DT[:], strictm[:], ALU.mult, [B_DT, B_c], [B_DTs])
            TTo("dve", DTsb[:].rearrange("p (h c) -> p h c", h=8), DTs[:].rearrange("p (h c) -> p h c", h=8),
                bass.AP(sm, NEGB * 8, [[128, 128], [1, 8], [0, 128]]), ALU.mult, [B_DTs, B_sm], [B_DTsb])
            yield
            bt = bank()
            for h in range(8):
                P.op("pe", lambda e, o_=psb[bt][:, h * 128:(h + 1) * 128], i_=qkvr[:, 8 + h, ts_]:
                     e.transpose(out=o_, in_=i_, identity=ident[:]), [B_q[8 + h], B_c], [B_ps[bt]])
            pv = psb[bt][:, 0:1024].rearrange("p (h c) -> p h c", h=8)
            TTo("dve", GK[:], pv, bass.AP(sm, GS * 8, [[128, 128], [1, 8], [0, 128]]), ALU.mult, [B_ps[bt], B_sm], [B_GK])
            TTo("dve", Kd[:], pv, bass.AP(sm, EGD * 8, [[128, 128], [1, 8], [0, 128]]), ALU.mult, [B_ps[bt], B_sm], [B_Kd])
            bt = bank()
            for h in range(8):
                P.op("pe", lambda e, o_=psb[bt][:, h * 128:(h + 1) * 128], i_=qkvr[:, 16 + h, ts_]:
                     e.transpose(out=o_, in_=i_, identity=ident[:]), [B_q[16 + h], B_c], [B_ps[bt]])
            CP("act", Vt[:], psb[bt][:, 0:1024].rearrange("p (h c) -> p h c", h=8), [B_ps[bt]], [B_Vt])
            yield
            bkk = [bank(), bank()]
            bqk = [bank(), bank()]
            for h in range(8):
                kT = qkvr[:, 8 + h, ts_]
                qT = qkvr[:, h, ts_]
                sl = slice((h % 4) * 128, (h % 4 + 1) * 128)
                MM(ps[bkk[h // 4]][:, sl], kT, kT, True, True, [B_q[8 + h]], [B_ps[bkk[h // 4]]])
                MM(ps[bqk[h // 4]][:, sl], kT, qT, True, True, [B_q[8 + h], B_q[h]], [B_ps[bqk[h // 4]]])
            for i in range(2):
                fs = slice(i * 512, (i + 1) * 512)
                TTo("dve", Xs[0][:, fs], ps[bkk[i]][:, :], DTsb[:, fs], ALU.mult, [B_ps[bkk[i]], B_DTsb], [B_X[0][i]])
                TTo("dve", QKm[:, fs], ps[bqk[i]][:, :], DT[:, fs], ALU.mult, [B_ps[bqk[i]], B_DT], [B_QKm])
            bt = bank()
            for h in range(8):
                P.op("pe", lambda e, o_=psb[bt][:, h * 128:(h + 1) * 128], i_=Xs[0][:, h * 128:(h + 1) * 128]:
                     e.transpose(out=o_, in_=i_, identity=ident[:]), [B_X[0][h // 4], B_c], [B_ps[bt]])
            CP("act", Ys[0][:], psb[bt][:, 0:1024], [B_ps[bt]], B_Y[0])
            CP("pool", Yo[64:128, :, :], Ys[0][64:128, :].rearrange("p (h c) -> p h c", h=8)[:, :, 0:64], B_Y[0], [B_Yo])
            MEMSET("pool", Ys[0][64:128, :].rearrange("p (h c) -> p h c", h=8)[:, :, 0:64], 0.0, B_Y[0])
            MEMSET("pool", Xs[0][0:64, :].rearrange("p (h c) -> p h c", h=8)[:, :, 64:128], 0.0, B_X[0])
            for i in range(2):
                fs = slice(i * 512, (i + 1) * 512)
                TTo("dve", Ws[0][:, fs], Xs[0][:, fs], identt[:, fs], ALU.add, [B_X[0][i], B_c], [B_W[0][i]])
            yield
            for lv in range(1, 6):
                pi, ci = (lv - 1) % 2, lv % 2
                by = [bank(), bank()]
                bx = [bank(), bank()] if lv < 5 else None
                for h in range(8):
                    sl = slice((h % 4) * 128, (h % 4 + 1) * 128)
                    hsl = slice(h * 128, (h + 1) * 128)
                    MM(ps[by[h // 4]][:, sl], Xs[pi][:, hsl], Ys[pi][:, hsl], True, True, [B_X[pi][h // 4], B_Y[pi][h // 4]],
                       [B_ps[by[h // 4]]])
                    if bx is not None:
                        MM(ps[bx[h // 4]][:, sl], Ys[pi][:, hsl], Xs[pi][:, hsl], True, True, [B_X[pi][h // 4], B_Y[pi][h // 4]],
                           [B_ps[bx[h // 4]]])
                for i in range(2):
                    fs = slice(i * 512, (i + 1) * 512)
                    CP("act", Ys[ci][:, fs], ps[by[i]][:, :], [B_ps[by[i]]], [B_Y[ci][i]])
                    if bx is not None:
                        CP("dve", Xs[ci][:, fs], ps[bx[i]][:, :], [B_ps[bx[i]]], [B_X[ci][i]])
                yield
                bw = [bank(), bank()]
                for h in range(8):
                    sl = slice((h % 4) * 128, (h % 4 + 1) * 128)
                    hsl = slice(h * 128, (h + 1) * 128)
                    MM(ps[bw[h // 4]][:, sl], Ys[ci][:, hsl], Ws[pi][:, hsl], True, True, [B_W[pi][h // 4], B_Y[ci][h // 4]],
                       [B_ps[bw[h // 4]]])
                TTo("dve", Ws[ci][:, 0:512], ps[bw[0]][:, :], Ws[pi][:, 0:512], ALU.add, [B_ps[bw[0]], B_W[pi][0]], [B_W[ci][0]])
                TTo("dve", Ws[ci][:, 512:1024], ps[bw[1]][:, :], Ws[pi][:, 512:1024], ALU.add, [B_ps[bw[1]], B_W[pi][1]], [B_W[ci][1]])
                yield
            Wd, B_Wd = Ws[1], B_W[1]
            bp = bank()
            bz = bank()
            for h in range(8):
                MM(ps[bp][0:64, h * 64:(h + 1) * 64], Yo[64:128, h, :], Wd[64:128, h * 128 + 64:(h + 1) * 128], True, True,
                   [B_Yo, B_Wd[h // 4]], [B_ps[bp]])
                P.op("pe", lambda e, o_=psb[bz][0:64, h * 64:(h + 1) * 64], i_=Wd[0:64, h * 128:h * 128 + 64]:
                     e.transpose(out=o_, in_=i_, identity=ident[0:64, 0:64]), [B_Wd[h // 4], B_c], [B_ps[bz]])
            P1sb = Xs[0][0:64, 0:512]
            Zsb = Xs[1][0:64, 0:512]
            CP("act", P1sb, ps[bp][0:64, :], [B_ps[bp]], [B_X[0][0]])
            CP("dve", Zsb, psb[bz][0:64, 0:512], [B_ps[bz]], [B_X[1][0]])
            yield
            bw12 = bank()
            for h in range(8):
                MM(ps[bw12][0:64, h * 64:(h + 1) * 64], Zsb[:, h * 64:(h + 1) * 64], P1sb[:, h * 64:(h + 1) * 64], True, True,
                   [B_X[0][0], B_X[1][0]], [B_ps[bw12]])
            CP("dve", Wd[0:64, :].rearrange("p (h c) -> p h c", h=8)[:, :, 64:128],
               ps[bw12][0:64, :].rearrange("p (h c) -> p h c", h=8), [B_ps[bw12]], B_Wd)
            DBG('dbg_T2T', T2T[:], B_T2T, t, cc == 0)
            DBG('dbg_DT', DT[:], [B_DT], t, cc == 0)
            DBG('dbg_sm', sm[:], [B_sm], t, cc == 0)
            yield
            bww = [bank(), bank()]
            for h in range(8):
                sl = slice((h % 4) * 128, (h % 4 + 1) * 128)
                MM(ps[bww[h // 4]][:, sl], GK[:, h, :], T2T[:, h * 128:(h + 1) * 128], True, True, [B_GK, B_T2T[h // 4]],
                   [B_ps[bww[h // 4]]])
            for i in range(2):
                ACT(nww[:, i * 4:(i + 1) * 4, :], ps[bww[i]][:, :].rearrange("p (h c) -> p h c", h=4), AF.Copy,
                    [B_ps[bww[i]]], [B_nww], scale=-1.0)
            yield

        def chunk_B(cc, S):
            ts_ = slice(4 + cc * 128, 4 + (cc + 1) * 128)
            hs_ = slice(cc * 128, (cc + 1) * 128)
            sm, gsb, ldU, DT, DTs, DTsb, QKm = smc[cc], gsbc[cc], S["ldU"], S["DT"], S["DTs"], S["DTsb"], S["QKm"]
            on, B_on = S["on"], S["B_on"]
            GK, Kd, Vt, nww, Xs, Ys, Ws = S["GK"], S["Kd"], S["Vt"], S["nww"], S["Xs"], S["Ys"], S["Ws"]
            B_sm, B_ldU, B_DT, B_DTs, B_DTsb, B_QKm = B_smc[cc], S["B_ldU"], S["B_DT"], S["B_DTs"], S["B_DTsb"], S["B_QKm"]
            B_GK, B_Kd, B_Vt, B_nww, B_X, B_Y, B_W = S["B_GK"], S["B_Kd"], S["B_Vt"], S["B_nww"], S["B_X"], S["B_Y"], S["B_W"]
            T2T = Ws[1]
            B_T2T = B_W[1]
            Yo, B_Yo = S["Yo"], S["B_Yo"]

            def smv(i):
                return sm[:, i, :]

            bv = [bank(), bank()]
            for h in range(8):
                sl = slice((h % 4) * 128, (h % 4 + 1) * 128)
                MM(ps[bv[h // 4]][:, sl], T2T[:, h * 128:(h + 1) * 128], Vt[:, h, :], True, False, [B_T2T[h // 4], B_Vt],
                   [B_ps[bv[h // 4]]])
                MM(ps[bv[h // 4]][:, sl], nww[:, h, :], S_b[:, h, :], False, True, [B_nww, B_Sb], [B_ps[bv[h // 4]]])
            for i in range(2):
                TTo("dve", vn[:, i * 4:(i + 1) * 4, :], ps[bv[i]][:, :].rearrange("p (h c) -> p h c", h=4),
                    bass.AP(sm, BET * 8 + i * 4, [[128, 128], [1, 4], [0, 128]]), ALU.mult, [B_ps[bv[i]], B_sm], [B_vn])
            yield
            bo1 = [bank(), bank()]
            bo2 = [bank(), bank()]
            for h in range(8):
                sl = slice((h % 4) * 128, (h % 4 + 1) * 128)
                MM(ps[bo1[h // 4]][:, sl], qkvr[:, h, ts_], S_b[:, h, :], True, True, [B_q[h], B_Sb], [B_ps[bo1[h // 4]]])
                MM(ps[bo2[h // 4]][:, sl], QKm[:, h * 128:(h + 1) * 128], vn[:, h, :], True, True, [B_QKm, B_vn],
                   [B_ps[bo2[h // 4]]])
            for i in range(2):
                TTo("dve", of[:, i * 4:(i + 1) * 4, :], ps[bo1[i]][:, :].rearrange("p (h c) -> p h c", h=4),
                    bass.AP(sm, GS * 8 + i * 4, [[128, 128], [1, 4], [0, 128]]), ALU.mult, [B_ps[bo1[i]], B_sm], [B_of])
            for i in range(2):
                TTo("dve", of[:, i * 4:(i + 1) * 4, :], of[:, i * 4:(i + 1) * 4, :],
                    ps[bo2[i]][:, :].rearrange("p (h c) -> p h c", h=4), ALU.add, [B_ps[bo2[i]], B_of], [B_of])
            DBG('dbg_of', of[:], [B_of], t, cc == 0)
            DBG('dbg_vn', vn[:], [B_vn], t, cc == 0)
            yield
            bs = [bank(), bank()]
            for h in range(8):
                sl = slice((h % 4) * 128, (h % 4 + 1) * 128)
                MM(ps[bs[h // 4]][:, sl], Kd[:, h, :], vn[:, h, :], True, True, [B_Kd, B_vn], [B_ps[bs[h // 4]]])
            TTo("dve", S_f[:], S_f[:], bass.AP(sm, EGL * 8, [[128, 128], [1, 8], [0, 128]]), ALU.mult, [B_Sf, B_sm],
                [B_Sf])
            for i in range(2):
                TTo("dve", S_f[:, i * 4:(i + 1) * 4, :], S_f[:, i * 4:(i + 1) * 4, :],
                    ps[bs[i]][:, :].rearrange("p (h c) -> p h c", h=4), ALU.add, [B_ps[bs[i]], B_Sf], [B_Sf])
            CP("act", S_b[:], S_f[:], [B_Sf], [B_Sb])
            yield
            ofl = of[:].rearrange("p h c -> p (h c)")
            ACT(on[:], ofl, AF.Square, [B_of], [B_on])
            P.op("dve", lambda e, o_=smv(SSQ), i_=on[:].rearrange("p (h c) -> p h c", h=8):
                 e.tensor_reduce(out=o_, in_=i_, axis=mybir.AxisListType.X, op=ALU.add), [B_on], [B_sm])
            ACT(smv(RSO), smv(SSQ), AF.Ln, [B_sm], [B_sm], scale=1.0 / 128, bias=EPS)
            ACT(smv(RSO), smv(RSO), AF.Exp, [B_sm], [B_sm], scale=-0.5)
            TTo("dve", on[:].rearrange("p (h c) -> p h c", h=8), of[:],
                bass.AP(sm, RSO * 8, [[128, 128], [1, 8], [0, 128]]), ALU.mult, [B_of, B_sm], [B_on])
            yield

        def chunk_T(cc, S):
            hs_ = slice(cc * 128, (cc + 1) * 128)
            on, B_on = S["on"], S["B_on"]
            bt = bank()
            for h in range(8):
                P.op("pe", lambda e, o_=psb[bt][:, h * 128:(h + 1) * 128], i_=on[:, h * 128:(h + 1) * 128]:
                     e.transpose(out=o_, in_=i_, identity=ident[:]), [B_on, B_c], [B_ps[bt]])
            STT(ogT[:, :, hs_], psb[bt][:, 0:1024].rearrange("p (h c) -> p h c", h=8), dng[:, 0:1], sz[:, :, hs_],
                ALU.mult, ALU.mult, [B_ps[bt], B_c] + B_sz, [B_og[cc]])
            yield

        def m2():
            for g in (10, 11):
                wt, wb = w_get(("in", g))
                for j in range(4):
                    c = (g - 10) * 4 + j
                    b = bank()
                    for k in range(8):
                        MM(ps[b][:, :], wt[:, k, j * 128:(j + 1) * 128], hT[:, k, :], k == 0, k == 7, [wb] + B_hT, [B_ps[b]])
                    ACT(sgb[:, c, :], ps[b][:, :], AF.Sigmoid, [B_ps[b]], [B_sgb[c]])
                    yield
            for g in (8, 9):
                wt, wb = w_get(("in", g))
                for j in range(4):
                    c = (g - 8) * 4 + j
                    b = bank()
                    for k in range(8):
                        MM(ps[b][:, :], wt[:, k, j * 128:(j + 1) * 128], hT[:, k, :], k == 0, k == 7, [wb] + B_hT, [B_ps[b]])
                    TTo("dve", cbuf[:, c, 32:32 + TT], ps[b][:, :], sgb[:, c, :], ALU.mult, [B_ps[b], B_sgb[c]], [B_cb[c]])
                    yield
            for g in (12, 13):
                wt, wb = w_get(("in", g))
                for j in range(4):
                    c = (g - 12) * 4 + j
                    b = bank()
                    for k in range(8):
                        MM(ps[b][:, :], wt[:, k, j * 128:(j + 1) * 128], hT[:, k, :], k == 0, k == 7, [wb] + B_hT, [B_ps[b]])
                    ACT(szc[:, c, :], ps[b][:, :], AF.Silu, [B_ps[b]], [B_szc[c]])
                    yield
            for c in range(8):
                wt, wb = w_get(("dg", c))
                wflat = wt[:].rearrange("p k n -> p (k n)")
                b = 7
                for j in range(31):
                    MM(ps[b][:, :], wflat[:, j * 128:(j + 1) * 128], cbuf[:, c, 2 + j:2 + j + TT], j == 0, j == 30,
                       [wb, B_cb[c]], [B_ps[b]])
                    if j % 8 == 7:
                        yield
                ACT(cvb[:, c, :], ps[b][:, :], AF.Identity, [B_ps[b], B_c], [B_cv[c]], bias=cfb[:, c:c + 1])
                CP("pool", cbuf[:, c, 2:32], cbuf[:, c, 2 + TT:32 + TT], [B_cb[c]], [B_cb[c]])
                yield

        def seq(*gens):
            for g_ in gens:
                yield from g_

        RATE = 56.0 / 58.0
        pace = [0.0, 0, 0]

        def run_with(bg, gens):
            gens = list(gens)
            while gens:
                for g_ in list(gens):
                    try:
                        next(g_)
                    except StopIteration:
                        gens.remove(g_)
                pace[1] += 1
                pace[0] += RATE
                while pace[0] >= 1.0:
                    pace[0] -= 1.0
                    try:
                        next(bg)
                        pace[2] += 1
                    except StopIteration:
                        pass

        gm = m2()
        pendT = []
        for c0 in (0, 2):
            run_with(gm, [chunk_A(c0, SETS[0]), chunk_A(c0 + 1, SETS[1])] + ([seq(*pendT)] if pendT else []))
            run_with(gm, [seq(chunk_B(c0, SETS[0]), chunk_B(c0 + 1, SETS[1]))])
            pendT = [chunk_T(c0, SETS[0]), chunk_T(c0 + 1, SETS[1])]
        run_with(gm, [seq(*pendT)])
        for _ in gm:
            pace[2] += 1
        if t == 0:
            pass
        DBG('dbg_og', ogT[:], B_og, t)
        ph.end()

        ph = Phase()
        cgT = ph.sb("cgT", [128, 8, TT], BF16)
        B_cg = ph.bufs_n("cg", 8)
        sgd = ph.sb("sgd", [128, 8, TT], BF16)
        sgc = ph.sb("sgc", [128, 8, TT], BF16)
        B_sgd = ph.bufs_n("sgd", 8)
        B_sgc = ph.bufs_n("sgc", 8)
        mT = ph.sb("mT", [128, 8, TT], BF16)
        B_m = ph.bufs_n("m", 8)
        tmpm = [ph.sb("tmpm%d" % i, [128, TT], F32) for i in range(2)]
        B_tmpm = ph.bufs_n("tmpm", 2)
        pstg = ph.sb("pstg", [128, 4, 256], F32)
        B_pstg = ph.bufs_n("pstg", 4)
        pbf = ph.sb("pbf", [128, 256], BF16)
        B_pbf = ph.buf("pbf")
        pT = ph.sb("pT", [128, 2, TT], BF16)
        B_pT = ph.bufs_n("pT", 4)
        sgp = [ph.sb("sgp%d" % i, [128, 512], F32) for i in range(2)]
        B_sgp = ph.bufs_n("sgp", 2)
        tmpe = [ph.sb("tmpe%d" % i, [128, 512], F32) for i in range(2)]
        B_tmpe = ph.bufs_n("tmpe", 2)
        ob = [ph.sb("ob%d" % i, [128, D], F32) for i in range(2)]
        B_ob = ph.bufs_n("ob", 2)
        xtm = ph.sb("xtm", [128, 4, D], F32)
        B_x = ph.bufs_n("x", 4)
        hb = [ph.sb("hb%d" % i, [128, D], BF16) for i in range(2)]
        B_hb = ph.bufs_n("hb", 2)
        junk = ph.sb("junk", [128, D], BF16)
        B_junk = ph.buf("junk")
        ph.begin()
        for cc in range(4):
            P.dma("sp", xtm[:, cc, :], x_d[r0 + cc * 128:r0 + (cc + 1) * 128, :], writes=[B_x[cc]])
        for cc in range(4):
            P.dma("sp", pstg[:, cc, :], p_d[r0 + cc * 128:r0 + (cc + 1) * 128, :], writes=[B_pstg[cc]])
        phL = Phase()
        sq2 = [phL.sb("sq2%d" % i, [128, TT], BF16) for i in range(2)]
        B_sq2 = phL.bufs_n("sq2", 2)
        mean = phL.sb("mean", [128, TT], F32)
        msq = phL.sb("msq", [128, TT], F32)
        rstd = phL.sb("rstd", [128, TT], F32)
        B_mean, B_msq, B_rstd = phL.buf("mean"), phL.buf("msq"), phL.buf("rstd")
        xm = [phL.sb("xm%d" % i, [128, TT], F32) for i in range(2)]
        B_xm = phL.bufs_n("xm", 2)
        sl_ = [phL.sb("sl%d" % i, [128, TT], BF16) for i in range(2)]
        B_sl = phL.bufs_n("sl", 2)
        phL.begin()
        DBG('dbg_cv', cvb[:], B_cv, t)
        b1 = bank()
        b2 = bank()
        for c in range(8):
            a = c % 2
            MM(ps[b1][:, :], ones_bf[:], cvb[:, c, :], c == 0, c == 7, [B_cv[c], B_c], [B_ps[b1]])
            ACT(sq2[a][:], cvb[:, c, :], AF.Square, [B_cv[c]], [B_sq2[a]])
            MM(ps[b2][:, :], ones_bf[:], sq2[a][:], c == 0, c == 7, [B_sq2[a], B_c], [B_ps[b2]])
        ACT(mean[:], ps[b1][:, :], AF.Copy, [B_ps[b1]], [B_mean], scale=1.0 / 1024)
        ACT(msq[:], mean[:], AF.Square, [B_mean], [B_msq])
        STT(rstd[:], ps[b2][:, :], 1.0 / 1024, msq[:], ALU.mult, ALU.subtract, [B_ps[b2], B_msq], [B_rstd])
        ACT(rstd[:], rstd[:], AF.Ln, [B_rstd], [B_rstd], bias=EPS)
        ACT(rstd[:], rstd[:], AF.Exp, [B_rstd], [B_rstd], scale=-0.5)
        def g_pr():
            for g in (14, 15):
                def ev(j, b, g=g):
                    c = (g - 14) * 4 + j
                    ACT(sgd[:, c, :], ps[b][:, :], AF.Sigmoid, [B_ps[b]], [B_sgd[c]])
                yield from proj_units(("in", g), hT, B_hT, ev)
            for g in (16, 17):
                def ev(j, b, g=g):
                    c = (g - 16) * 4 + j
                    ACT(sgc[:, c, :], ps[b][:, :], AF.Sigmoid, [B_ps[b]], [B_sgc[c]])
                yield from proj_units(("in", g), hT, B_hT, ev)
            for g in range(2):
                def ev(j, b, g=g):
                    c = g * 4 + j
                    TTo("dve", mT[:, c, :], ps[b][:, :], sgd[:, c, :], ALU.mult, [B_ps[b], B_sgd[c]], [B_m[c]])
                yield from proj_units(("dn", g), ogT, B_og, ev)

        def g_ln():
            for c in range(8):
                a = c % 2
                TTo("dve", xm[a][:], cvb[:, c, :], mean[:], ALU.subtract, [B_cv[c], B_mean], [B_xm[a]])
                TTo("dve", xm[a][:], xm[a][:], rstd[:], ALU.mult, [B_xm[a], B_rstd], [B_xm[a]])
                ACT(sl_[a][:], xm[a][:], AF.Silu, [B_xm[a], B_c], [B_sl[a]], scale=lng[:, c:c + 1], bias=lnb[:, c:c + 1])
                TTo("dve", cgT[:, c, :], sl_[a][:], szc[:, c, :], ALU.mult, [B_sl[a], B_szc[c]], [B_cg[c]])
                yield

        gp, gl = g_pr(), g_ln()
        nunit = 0
        for _ in gp:
            nunit += 1
            if nunit >= 6 and nunit % 2 == 0:
                next(gl, None)
        for _ in gl:
            pass
        DBG('dbg_cg', cgT[:], B_cg, t)
        phL.end()
        for g in range(2):
            def ev(j, b, g=g):
                c = g * 4 + j
                a = c % 2
                TTo("dve", tmpm[a][:], ps[b][:, :], sgc[:, c, :], ALU.mult, [B_ps[b], B_sgc[c]], [B_tmpm[a]])
                TTo("dve", mT[:, c, :], mT[:, c, :], tmpm[a][:], ALU.add, [B_m[c], B_tmpm[a]], [B_m[c]])
            proj_fm(("cf", g), cgT, B_cg, ev)
        Hn = head_alloc(t + 1) if t + 1 < n_tiles else None
        for cc in range(4):
            CP("pool", pbf[:], pstg[:, cc, :], [B_pstg[cc]], [B_pbf])
            b = bank()
            for k in range(2):
                P.op("pe", lambda e, o_=psb[b][:, k * 128:(k + 1) * 128], i_=pbf[:, k * 128:(k + 1) * 128]:
                     e.transpose(out=o_, in_=i_, identity=ident[:]), [B_pbf, B_c], [B_ps[b]])
            CP("act", pT[:, :, cc * 128:(cc + 1) * 128], psb[b][:, 0:256].rearrange("p (k t) -> p k t", k=2),
               [B_ps[b]], [B_pT[cc]])
        wts = [w_get(("out", 0)), w_get(("out", 1), hold=1)]
        def ple_T(cc):
            i = cc % 2
            b = bank()
            for k in range(8):
                P.op("pe", lambda e, o_=psb[b][:, k * 128:(k + 1) * 128], i_=hb[i][:, k * 128:(k + 1) * 128]:
                     e.transpose(out=o_, in_=i_, identity=ident[:]), [B_hb[i], B_c], [B_ps[b]])
            TTo("dve", hT[:, :, cc * 128:(cc + 1) * 128], psb[b][:, 0:1024].rearrange("p (k t) -> p k t", k=8),
                bass.AP(png, 0, [[8, 128], [1, 8], [0, 128]]), ALU.mult, [B_ps[b], B_c], [B_hT[cc]])

        for cc in range(4):
            for nh in range(2):
                wt, wb = wts[nh]
                b = bank()
                for k in range(8):
                    MM(ps[b][:, :], mT[:, k, cc * 128:(cc + 1) * 128], wt[:, k, :], k == 0, k == 7, [wb] + B_m, [B_ps[b]])
                xs = xtm[:, cc, nh * 512:(nh + 1) * 512]
                TTo("dve", xs, xs, ps[b][:, :], ALU.add, [B_ps[b], B_x[cc]], [B_x[cc]])
            i = cc % 2
            ACT(junk[:], xtm[:, cc, :], AF.Square, [B_x[cc]], [B_junk, B_stp[cc]], accum_out=st4p[:, cc:cc + 1])
            ACT(st4p[:, 4 + cc:5 + cc], st4p[:, cc:cc + 1], AF.Ln, [B_stp[cc]], [B_stp[cc]], scale=1.0 / D, bias=EPS)
            ACT(st4p[:, 4 + cc:5 + cc], st4p[:, 4 + cc:5 + cc], AF.Exp, [B_stp[cc]], [B_stp[cc]], scale=-0.5)
            if cc >= 1:
                ple_T(cc - 1)
            ACT(hb[i][:], xtm[:, cc, :], AF.Copy, [B_x[cc], B_stp[cc]], [B_hb[i]], scale=st4p[:, 4 + cc:5 + cc])
        ple_T(3)
        DBG('dbg_m', mT[:], B_m, t)
        DBG('dbg_x1', xtm[:], B_x, t)
        if Hn is not None:
            head_act(Hn)
        for nh in range(2):
            if nh == 1 and Hn is not None:
                head_pe(Hn)
            wg, wgb = w_get(("pg", nh))
            wp, wpb = w_get(("pp", nh), hold=1)
            for cc in range(4):
                a = cc % 2
                b = bank()
                for k in range(8):
                    MM(ps[b][:, :], hT[:, k, cc * 128:(cc + 1) * 128], wg[:, k, :], k == 0, k == 7, [wgb, B_hT[cc]], [B_ps[b]])
                ACT(sgp[a][:], ps[b][:, :], AF.Sigmoid, [B_ps[b]], [B_sgp[a]])
                b = bank()
                for k in range(2):
                    MM(ps[b][:, :], pT[:, k, cc * 128:(cc + 1) * 128], wp[:, k, :], k == 0, k == 1, [wpb, B_pT[cc]], [B_ps[b]])
                TTo("dve", tmpe[a][:], ps[b][:, :], sgp[a][:], ALU.mult, [B_ps[b], B_sgp[a]], [B_tmpe[a]])
                xs = xtm[:, cc, nh * 512:(nh + 1) * 512]
                TTo("dve", xs, xs, tmpe[a][:], ALU.add, [B_tmpe[a], B_x[cc]], [B_x[cc]])
        for cc in range(4):
            ACT(junk[:], xtm[:, cc, :], AF.Square, [B_x[cc]], [B_junk, B_st4], accum_out=st4[:, cc:cc + 1])
        ACT(st4[:, 4:8], st4[:, 0:4], AF.Ln, [B_st4], [B_st4], scale=1.0 / D, bias=EPS)
        ACT(st4[:, 4:8], st4[:, 4:8], AF.Exp, [B_st4], [B_st4], scale=-0.5)
        for cc in range(4):
            a = cc % 2
            STT(ob[a][:], xtm[:, cc, :], st4[:, 4 + cc:5 + cc], fg[:], ALU.mult, ALU.mult, [B_x[cc], B_st4, B_c], [B_ob[a]])
            out_dmas.append(P.dma("pool", y_d[r0 + cc * 128:r0 + (cc + 1) * 128, :], ob[a][:], reads=[B_ob[a]]))
        ph.end()
        phC.end()

    P.emit(final_wait_ops=out_dmas)
    pst.close()
    return nc


def _consts():
    bf = ml_dtypes.bfloat16
    j = np.arange(128)
    c = {}
    c["c_ident"] = np.eye(128, dtype=np.float32).astype(bf)
    c["c_ones"] = np.ones((128, 128), np.float32).astype(bf)
    c["c_onesf"] = np.ones((128, 128), np.float32)
    c["c_ltri"] = (j[:, None] <= j[None, :]).astype(np.float32)
    c["c_ltrib"] = c["c_ltri"].astype(bf)
    ugt = (j[:, None] > j[None, :]).astype(np.float32)
    c["c_ugt"] = np.ascontiguousarray(np.broadcast_to(ugt[:, None, :], (128, 8, 128)))
    c["c_negm"] = np.tile(np.where(j[None, :] < j[:, None], -30000.0, 0.0).astype(np.float32), (1, 4)).astype(bf)
    strict = (j[None, :] > j[:, None]).astype(np.float32)
    c["c_strict"] = np.tile(strict, (1, 8)).astype(bf)
    c["c_identt"] = np.tile(np.eye(128, dtype=np.float32), (1, 8)).astype(bf)
    esel = np.zeros((128, 16, 64), np.float32)
    bsel = np.zeros((64, 16, 128), np.float32)
    for i in range(16):
        row = i if i < 8 else 32 + (i - 8)
        esel[:, i, row] = 1.0
        bsel[row, i, :] = 1.0
    c["c_esel"] = esel.astype(bf)
    c["c_bsel"] = bsel.astype(bf)
    return c


def _pk(v):
    return np.ascontiguousarray(np.asarray(v, np.float32).reshape(8, 128).T)


_NC_CACHE = {}


def kernel(x, p, mix_norm_g, w_in, dn_conv_w, dn_a_log, dn_dt_bias, dn_out_norm_g, w_dn_out,
           cf_dw_w, cf_dw_b, cf_ln_g, cf_ln_b, w_cf_out, w_out, ple_norm_g, w_ple_gate,
           w_ple_proj, final_norm_g):
    x = np.asarray(x, np.float32)
    p = np.asarray(p, np.float32)
    n_cores = 8
    if "nc" not in _NC_CACHE:
        _NC_CACHE["nc"] = build_program()
    nc = _NC_CACHE["nc"]
    f = lambda a: np.ascontiguousarray(np.asarray(a, np.float32))
    shared = dict(
        w_in=f(w_in[0]), w_dn_out=f(w_dn_out[0]), w_cf_out=f(w_cf_out[0]), w_out=f(w_out[0]),
        w_ple_gate=f(w_ple_gate[0]), w_ple_proj=f(w_ple_proj[0]),
        mix_norm_g=_pk(mix_norm_g[0]), ple_norm_g=_pk(ple_norm_g[0]),
        dn_out_norm_g=f(np.asarray(dn_out_norm_g[0]).reshape(128, 1)),
        dn_conv_w=f(np.asarray(dn_conv_w[0]).reshape(4, 24, 128).transpose(2, 1, 0)),
        cf_dw_w=f(np.asarray(cf_dw_w[0]).reshape(31, 8, 128).transpose(2, 1, 0)),
        cf_dw_b=_pk(cf_dw_b[0]), cf_ln_g=_pk(cf_ln_g[0]), cf_ln_b=_pk(cf_ln_b[0]),
        dn_a_log=f(np.broadcast_to(np.asarray(dn_a_log[0])[None, :], (128, 8))),
        dn_dt_bias=f(np.broadcast_to(np.asarray(dn_dt_bias[0])[None, :], (128, 8))),
        final_norm_g=f(np.broadcast_to(np.asarray(final_norm_g)[None, :], (128, D))),
    )
    shared.update(_consts())
    in_maps = []
    for b in range(n_cores):
        m = dict(shared)
        m["x"] = f(x[b])
        m["p"] = f(p[0, b])
        in_maps.append(m)
    res = run_bass_kernel_spmd(nc, in_maps, core_ids=list(range(n_cores)))
    out = np.stack([np.asarray(r["y"], np.float32) for r in res.results], axis=0)
    return out
```
